# Optimizing a Trainium2 kernel written in Bass

```python
import math, functools
import jax, jax.numpy as jnp
from jax import lax
import numpy as np

D_MODEL = 2048
BATCH = 8
SEQ = 2048
DEPTH = 1
DEC_BATCH = 16
DEC_SEQ = 2048
PAST_LEN = 128

D_MIX = D_MODEL
W_RG = D_MIX // 2
RG_BLOCKS = 4
RG_BS = W_RG // RG_BLOCKS
RG_C = 8.0
CONV_W = 4
CONV_LEFT = 2
GLA_HEADS = 4
GLA_DV = (D_MIX - W_RG) // GLA_HEADS
GLA_DK = GLA_DV // 2
GK_RANK = 16
GATE_NORM = 16.0
CHUNK = 64
D_FF = 4 * D_MODEL
EPS = 1e-6
QK_W = GLA_HEADS * GLA_DK
V_W = GLA_HEADS * GLA_DV
P_IN = 2 * W_RG + 2 * QK_W + 2 * V_W + 2 * GK_RANK

kernel_name = 'hymba_style_rglru_gla_encoder'


def _rmsnorm(x, w):
    x32 = x.astype(jnp.float32)
    y = x32 * lax.rsqrt(jnp.mean(x32 * x32, axis=-1, keepdims=True) + EPS)
    return (y * w.astype(jnp.float32)).astype(x.dtype)


def _centred_dwconv(x, w, b):
    s = x.shape[1]
    xp = jnp.pad(x, ((0, 0), (CONV_LEFT, CONV_W - 1 - CONV_LEFT), (0, 0)))
    y = b
    for j in range(CONV_W):
        y = y + xp[:, j:j + s] * w[j]
    return y


def _lin_combine(c1, c2):
    a1, b1 = c1
    a2, b2 = c2
    return a1 * a2, a2 * b1 + b2


def _rglru(x, w_a, b_a, w_i, b_i, lam, reverse):
    bsz, s, _ = x.shape
    x32 = x.astype(jnp.float32)
    xb = x32.reshape(bsz, s, RG_BLOCKS, RG_BS)
    r = jax.nn.sigmoid(jnp.einsum('bsnc,ncd->bsnd', xb, w_a).reshape(bsz, s, W_RG) + b_a)
    i = jax.nn.sigmoid(jnp.einsum('bsnc,ncd->bsnd', xb, w_i).reshape(bsz, s, W_RG) + b_i)
    log_a = -RG_C * r * jax.nn.softplus(-lam.astype(jnp.float32))
    a = jnp.exp(log_a)
    u = jnp.sqrt(-jnp.expm1(2.0 * log_a)) * (i * x32)
    _, h = lax.associative_scan(_lin_combine, (a, u), reverse=reverse, axis=1)
    return h


def _gla_chunked(q, k, v, log_g):
    bsz, s, h, dk = q.shape
    dv = v.shape[-1]
    n = s // CHUNK
    q = q.reshape(bsz, n, CHUNK, h, dk) * (dk ** -0.5)
    k = k.reshape(bsz, n, CHUNK, h, dk)
    v = v.reshape(bsz, n, CHUNK, h, dv)
    bcum = jnp.cumsum(log_g.reshape(bsz, n, CHUNK, h, dk), axis=2)
    q_e = q * jnp.exp(bcum)
    k_e = k * jnp.exp(-bcum)
    mask = jnp.tril(jnp.ones((CHUNK, CHUNK), dtype=bool))
    attn = jnp.where(mask, jnp.einsum('bnihk,bnjhk->bnhij', q_e, k_e), 0.0)
    o_intra = jnp.einsum('bnhij,bnjhv->bnihv', attn, v)
    b_last = bcum[:, :, -1]
    k_s = k * jnp.exp(b_last[:, :, None] - bcum)
    kv = jnp.einsum('bnchk,bnchv->bnhkv', k_s, v)
    decay = jnp.exp(b_last)

    def step(state, inp):
        d, kv_c = inp
        return d[..., None] * state + kv_c, state

    s0 = jnp.zeros((bsz, h, dk, dv), jnp.float32)
    _, s_prev = lax.scan(step, s0, (jnp.moveaxis(decay, 1, 0), jnp.moveaxis(kv, 1, 0)))
    o_inter = jnp.einsum('bnchk,nbhkv->bnchv', q_e, s_prev)
    return (o_intra + o_inter).reshape(bsz, s, h, dv)


def _flip(t):
    return jnp.flip(t, axis=1)


def _layer(x, norm_mix_pre, w_in, rg_conv_w, rg_conv_b,
           rg_a_w_fwd, rg_a_b_fwd, rg_i_w_fwd, rg_i_b_fwd, rg_lambda_fwd,
           rg_a_w_bwd, rg_a_b_bwd, rg_i_w_bwd, rg_i_b_bwd, rg_lambda_bwd,
           gla_gk_w_fwd, gla_gk_b_fwd, gla_gk_w_bwd, gla_gk_b_bwd, gla_norm,
           w_out, norm_mix_post, norm_ffn_pre, w_up, w_down, norm_ffn_post):
    bsz, s, _ = x.shape
    f32 = jnp.float32
    h = _rmsnorm(x, norm_mix_pre)
    z = h @ w_in
    cuts = [W_RG, 2 * W_RG, 2 * W_RG + QK_W, 2 * W_RG + 2 * QK_W,
            2 * W_RG + 2 * QK_W + V_W, 2 * W_RG + 2 * QK_W + 2 * V_W,
            2 * W_RG + 2 * QK_W + 2 * V_W + GK_RANK]
    rg_x, rg_gate, q, k, v, g, lr_f, lr_b = jnp.split(z, cuts, axis=-1)

    xc = _centred_dwconv(rg_x, rg_conv_w, rg_conv_b)
    h_f = _rglru(xc, rg_a_w_fwd, rg_a_b_fwd, rg_i_w_fwd, rg_i_b_fwd, rg_lambda_fwd, False)
    h_b = _rglru(xc, rg_a_w_bwd, rg_a_b_bwd, rg_i_w_bwd, rg_i_b_bwd, rg_lambda_bwd, True)
    rg_out = (h_f + h_b).astype(x.dtype) * jax.nn.gelu(rg_gate)

    qh = q.reshape(bsz, s, GLA_HEADS, GLA_DK).astype(f32)
    kh = k.reshape(bsz, s, GLA_HEADS, GLA_DK).astype(f32)
    vh = v.reshape(bsz, s, GLA_HEADS, GLA_DV).astype(f32)
    lg_f = (jax.nn.log_sigmoid((lr_f @ gla_gk_w_fwd + gla_gk_b_fwd).astype(f32)) / GATE_NORM).reshape(bsz, s, GLA_HEADS, GLA_DK)
    lg_b = (jax.nn.log_sigmoid((lr_b @ gla_gk_w_bwd + gla_gk_b_bwd).astype(f32)) / GATE_NORM).reshape(bsz, s, GLA_HEADS, GLA_DK)
    o_f = _gla_chunked(qh, kh, vh, lg_f)
    o_b = _flip(_gla_chunked(_flip(qh), _flip(kh), _flip(vh), _flip(lg_b)))
    o = _rmsnorm(o_f + o_b, gla_norm).astype(x.dtype)
    gla_out = (o * jax.nn.silu(g.reshape(bsz, s, GLA_HEADS, GLA_DV))).reshape(bsz, s, V_W)

    mix = jnp.concatenate([rg_out, gla_out], axis=-1) @ w_out
    x = x + _rmsnorm(mix, norm_mix_post)

    hf = _rmsnorm(x, norm_ffn_pre)
    f = jnp.square(jax.nn.relu(hf @ w_up)) @ w_down
    return x + _rmsnorm(f, norm_ffn_post)


def _trunk(x, norm_mix_pre, w_in, rg_conv_w, rg_conv_b,
           rg_a_w_fwd, rg_a_b_fwd, rg_i_w_fwd, rg_i_b_fwd, rg_lambda_fwd,
           rg_a_w_bwd, rg_a_b_bwd, rg_i_w_bwd, rg_i_b_bwd, rg_lambda_bwd,
           gla_gk_w_fwd, gla_gk_b_fwd, gla_gk_w_bwd, gla_gk_b_bwd, gla_norm,
           w_out, norm_mix_post, norm_ffn_pre, w_up, w_down, norm_ffn_post):
    for l in range(DEPTH):
        x = _layer(x, norm_mix_pre[l], w_in[l], rg_conv_w[l], rg_conv_b[l],
                   rg_a_w_fwd[l], rg_a_b_fwd[l], rg_i_w_fwd[l], rg_i_b_fwd[l], rg_lambda_fwd[l],
                   rg_a_w_bwd[l], rg_a_b_bwd[l], rg_i_w_bwd[l], rg_i_b_bwd[l], rg_lambda_bwd[l],
                   gla_gk_w_fwd[l], gla_gk_b_fwd[l], gla_gk_w_bwd[l], gla_gk_b_bwd[l], gla_norm[l],
                   w_out[l], norm_mix_post[l], norm_ffn_pre[l], w_up[l], w_down[l], norm_ffn_post[l])
    return x


def setup_inputs(seed: int = 0) -> dict:
    key = jax.random.key(seed)
    ks = jax.random.split(key, 32)
    f32 = jnp.float32

    def nrm(k, shape, scale):
        return scale * jax.random.normal(k, shape, f32)

    def gain(k, n):
        return 1.0 + 0.05 * jax.random.normal(k, (DEPTH, n), f32)

    def lam(k):
        u = jax.random.uniform(k, (DEPTH, W_RG), f32, 0.9, 0.999)
        sg = u ** (1.0 / RG_C)
        return jnp.log(sg) - jnp.log1p(-sg)

    return {
        'x_prompt': jax.random.normal(ks[0], (BATCH, SEQ, D_MODEL), f32),
        'x_sample': jax.random.normal(ks[1], (DEC_BATCH, DEC_SEQ, D_MODEL), f32),
        'norm_mix_pre': gain(ks[2], D_MODEL),
        'w_in': nrm(ks[3], (DEPTH, D_MODEL, P_IN), D_MODEL ** -0.5),
        'rg_conv_w': nrm(ks[4], (DEPTH, CONV_W, W_RG), CONV_W ** -0.5),
        'rg_conv_b': nrm(ks[5], (DEPTH, W_RG), 0.02),
        'rg_a_w_fwd': nrm(ks[6], (DEPTH, RG_BLOCKS, RG_BS, RG_BS), RG_BS ** -0.5),
        'rg_a_b_fwd': nrm(ks[7], (DEPTH, W_RG), 0.02),
        'rg_i_w_fwd': nrm(ks[8], (DEPTH, RG_BLOCKS, RG_BS, RG_BS), RG_BS ** -0.5),
        'rg_i_b_fwd': nrm(ks[9], (DEPTH, W_RG), 0.02),
        'rg_lambda_fwd': lam(ks[10]),
        'rg_a_w_bwd': nrm(ks[11], (DEPTH, RG_BLOCKS, RG_BS, RG_BS), RG_BS ** -0.5),
        'rg_a_b_bwd': nrm(ks[12], (DEPTH, W_RG), 0.02),
        'rg_i_w_bwd': nrm(ks[13], (DEPTH, RG_BLOCKS, RG_BS, RG_BS), RG_BS ** -0.5),
        'rg_i_b_bwd': nrm(ks[14], (DEPTH, W_RG), 0.02),
        'rg_lambda_bwd': lam(ks[15]),
        'gla_gk_w_fwd': nrm(ks[16], (DEPTH, GK_RANK, QK_W), GK_RANK ** -0.5),
        'gla_gk_b_fwd': nrm(ks[17], (DEPTH, QK_W), 0.1),
        'gla_gk_w_bwd': nrm(ks[18], (DEPTH, GK_RANK, QK_W), GK_RANK ** -0.5),
        'gla_gk_b_bwd': nrm(ks[19], (DEPTH, QK_W), 0.1),
        'gla_norm': gain(ks[20], GLA_DV),
        'w_out': nrm(ks[21], (DEPTH, D_MIX, D_MODEL), D_MIX ** -0.5),
        'norm_mix_post': gain(ks[22], D_MODEL),
        'norm_ffn_pre': gain(ks[23], D_MODEL),
        'w_up': nrm(ks[24], (DEPTH, D_MODEL, D_FF), D_MODEL ** -0.5),
        'w_down': nrm(ks[25], (DEPTH, D_FF, D_MODEL), D_FF ** -0.5),
        'norm_ffn_post': gain(ks[26], D_MODEL),
    }


def reference(x_prompt, x_sample, norm_mix_pre, w_in, rg_conv_w, rg_conv_b,
              rg_a_w_fwd, rg_a_b_fwd, rg_i_w_fwd, rg_i_b_fwd, rg_lambda_fwd,
              rg_a_w_bwd, rg_a_b_bwd, rg_i_w_bwd, rg_i_b_bwd, rg_lambda_bwd,
              gla_gk_w_fwd, gla_gk_b_fwd, gla_gk_w_bwd, gla_gk_b_bwd, gla_norm,
              w_out, norm_mix_post, norm_ffn_pre, w_up, w_down, norm_ffn_post):
    trunk = functools.partial(
        _trunk, norm_mix_pre=norm_mix_pre, w_in=w_in, rg_conv_w=rg_conv_w, rg_conv_b=rg_conv_b,
        rg_a_w_fwd=rg_a_w_fwd, rg_a_b_fwd=rg_a_b_fwd, rg_i_w_fwd=rg_i_w_fwd, rg_i_b_fwd=rg_i_b_fwd,
        rg_lambda_fwd=rg_lambda_fwd, rg_a_w_bwd=rg_a_w_bwd, rg_a_b_bwd=rg_a_b_bwd,
        rg_i_w_bwd=rg_i_w_bwd, rg_i_b_bwd=rg_i_b_bwd, rg_lambda_bwd=rg_lambda_bwd,
        gla_gk_w_fwd=gla_gk_w_fwd, gla_gk_b_fwd=gla_gk_b_fwd, gla_gk_w_bwd=gla_gk_w_bwd,
        gla_gk_b_bwd=gla_gk_b_bwd, gla_norm=gla_norm, w_out=w_out, norm_mix_post=norm_mix_post,
        norm_ffn_pre=norm_ffn_pre, w_up=w_up, w_down=w_down, norm_ffn_post=norm_ffn_post)
    y_prompt = trunk(x_prompt)
    y_sample = trunk(x_sample)
    return (y_prompt, y_sample)
```

```python
import numpy as np
import concourse.bass as bass
import concourse.mybir as mybir
from concourse.bass_utils import run_bass_kernel_spmd

F32 = mybir.dt.float32
BF16 = mybir.dt.bfloat16
AF = mybir.ActivationFunctionType
ALU = mybir.AluOpType

D = 2048
P_IN = 5152
DFF = 8192
NCORE = 8
EPS = 1e-6
ENG = ["pe", "act", "dve", "pool", "sp"]
import os
_EV = os.environ.get("EVSEL", "alt")


def EVAC_SEL(c):
    return {"alt": c % 2 == 0, "act": True, "dve": False}[_EV]


class Sched:
    def __init__(self, nc):
        self.nc = nc
        self.prog = {e: [] for e in ENG}
        self.semh = {}
        self.cnt = {}
        self.lastw = {}
        self.readers = {}
        self.known = {e: {} for e in ENG}
        for e in ["pe", "act", "dve", "pool"]:
            self._mk(e)

    def _mk(self, name):
        if name not in self.semh:
            self.semh[name] = self.nc.alloc_semaphore(name="s_" + name)
            self.cnt[name] = 0

    def op(self, eng, fn, reads=(), writes=(), signal=True, dsem=None):
        pr = [k for k in reads if isinstance(k, tuple) and k[0] == "ps"]
        if pr:
            reads = [k for k in reads if k not in pr]
            writes = list(writes) + pr
        deps = {}

        def add(tok):
            if tok is not None:
                deps[tok[0]] = max(deps.get(tok[0], 0), tok[1])

        for k in reads:
            add(self.lastw.get(k))
        for k in writes:
            add(self.lastw.get(k))
            for sn, v in self.readers.get(k, {}).items():
                add((sn, v))
        waits = []
        for sn, v in deps.items():
            if sn == "pe" and eng == "pe":
                continue
            if self.known[eng].get(sn, 0) >= v:
                continue
            self.known[eng][sn] = v
            waits.append((sn, v))
        if dsem is not None:
            self._mk(dsem)
            self.cnt[dsem] += 16
            tok = (dsem, self.cnt[dsem])
            rec = (waits, fn, dsem, 16)
        elif signal:
            self.cnt[eng] += 1
            tok = (eng, self.cnt[eng])
            rec = (waits, fn, eng, 1)
        else:
            tok = (eng, self.cnt[eng] + 1)
            rec = (waits, fn, None, 0)
        self.prog[eng].append(rec)
        for k in reads:
            r = self.readers.setdefault(k, {})
            r[tok[0]] = max(r.get(tok[0], 0), tok[1])
        for k in writes:
            self.lastw[k] = tok
            self.readers[k] = {}

    def barrier(self, keep=()):
        kept = {k: self.lastw[k] for k in keep if k in self.lastw}
        skip = {tok[0] for tok in kept.values()}
        for e in ENG:
            waits = []
            for sn, c in self.cnt.items():
                if c == 0 or (sn == "pe" and e == "pe") or sn in skip:
                    continue
                if self.known[e].get(sn, 0) >= c:
                    continue
                self.known[e][sn] = c
                waits.append((sn, c))
            if waits:
                self.prog[e].append((waits, None, None, 0))
        self.lastw = dict(kept)
        self.readers = {}

    def replay(self, eng, e):
        for waits, fn, sn, inc in self.prog[eng]:
            for wsn, v in waits:
                e.wait_ge(self.semh[wsn], v)
            if fn is None:
                continue
            ins = fn(e)
            if sn is not None:
                ins.then_inc(self.semh[sn], inc)


def build(S, NSEQ, stop_after=None):
    nc = bass.Bass("TRN2", target_bir_lowering=False)
    NT = S // 128
    NB = S // 512
    T = NSEQ * S
    dt = nc.dram_tensor
    x = dt("x", [T, D], F32, kind="ExternalInput").ap()
    w_in = dt("w_in", [D, P_IN], F32, kind="ExternalInput").ap()
    w_out = dt("w_out", [D, D], F32, kind="ExternalInput").ap()
    w_up = dt("w_up", [D, DFF], F32, kind="ExternalInput").ap()
    w_down = dt("w_down", [DFF, D], F32, kind="ExternalInput").ap()
    rgw = dt("rgw", [4096, 256], F32, kind="ExternalInput").ap()
    pp = dt("pp", [128, 120], F32, kind="ExternalInput").ap()
    bvec = dt("bvec", [1, 4352], F32, kind="ExternalInput").ap()
    gkw = dt("gkw", [17, 1024], F32, kind="ExternalInput").ap()
    cst = dt("cst", [128, 7 * 128], F32, kind="ExternalInput").ap()
    y = dt("y", [T, D], F32, kind="ExternalOutput").ap()
    win_b = dt("win_b", [D, P_IN], BF16, kind="Internal").ap()
    wout_b = dt("wout_b", [D, D], BF16, kind="Internal").ap()
    wup_b = dt("wup_b", [D, DFF], BF16, kind="Internal").ap()
    wdown_b = dt("wdown_b", [DFF, D], BF16, kind="Internal").ap()
    rgw_b = dt("rgw_b", [4096, 256], BF16, kind="Internal").ap()
    mix_scr = dt("mix_scr", [NSEQ * 16 * 128, S], BF16, kind="Internal").ap()

    sb = nc.alloc_sbuf_tensor
    pp_t = sb("pp_t", [128, 120], F32)
    ppx = sb("ppx", [128, 16], F32)
    ppx2 = sb("ppx2", [128, 16], F32)
    cst_t = sb("cst_t", [128, 7 * 128], F32)
    idb = sb("idb", [128, 128], BF16)
    bv_t = sb("bv_t", [1, 512], F32)
    ones1 = sb("ones1", [1, 128], F32)
    wpost_b = sb("wpost_b", [128, 2048], F32)
    wfpost_b = sb("wfpost_b", [128, 2048], F32)
    gnorm_b = sb("gnorm_b", [128, 256], F32)
    gkw_t = sb("gkw_t", [64, 512], BF16)
    eps_t = sb("eps_t", [128, 1], F32)
    one_t = sb("one_t", [128, 1], F32)
    mhalf_t = sb("mhalf_t", [128, 1], F32)
    sm = sb("sm", [128, 32], F32)
    wbuf = [sb("wbuf%d" % i, [128, 16, 512], BF16) for i in range(3)]
    rgw_t = sb("rgw_t", [128, 4, 2, 256], BF16)
    wlr_t = sb("wlr_t", [128, 16, 32], BF16)
    regA = sb("regA", [128, 16 * 2048], BF16)
    WBYTES = 64 * 1024
    regW = sb("regW", [128, WBYTES // 4], F32)
    ps = nc.alloc_psum_tensor("ps", [128, 8, 512], F32)

    SC = Sched(nc)
    op = SC.op

    def MM(out, lhsT, rhs, start, stop, reads, writes, signal=True):
        op("pe", lambda e: e.matmul(out, lhsT, rhs, start=start, stop=stop), reads, writes, signal)

    def TR(out, in_, reads, writes, signal=True):
        op("pe", lambda e: e.transpose(out, in_, idb[:]), reads, writes, signal)

    def AC(out, in_, func, reads, writes, bias=None, scale=1.0, accum=None):
        kw = {}
        if bias is not None:
            kw["bias"] = bias
        if accum is not None:
            kw["accum_out"] = accum
        op("act", lambda e: e.activation(out=out, in_=in_, func=func, scale=scale, **kw), reads, writes)

    def TS(eng, out, in0, s1, s2, op0, op1, reads, writes):
        if op1 is None:
            op(eng, lambda e: e.tensor_scalar(out=out, in0=in0, scalar1=s1, scalar2=None, op0=op0), reads, writes)
        else:
            op(eng, lambda e: e.tensor_scalar(out=out, in0=in0, scalar1=s1, scalar2=s2, op0=op0, op1=op1), reads, writes)

    def STT(out, in0, scalar, in1, op0, op1, reads, writes):
        op("dve", lambda e: e.scalar_tensor_tensor(out=out, in0=in0, scalar=scalar, in1=in1, op0=op0, op1=op1), reads, writes)

    deferred = []
    pump_ctr = [0]

    def pump(n=1):
        for _ in range(n):
            if deferred:
                d_, s_, key = deferred.pop(0)
                DMA("pool", d_, s_, [], [key, ("castslot", cast_i[0] % 3)], "c_" + key)
                cast_i[0] += 1

    def TT(eng, out, in0, in1, aop, reads, writes):
        op(eng, lambda e: e.tensor_tensor(out=out, in0=in0, in1=in1, op=aop), reads, writes)
        if eng == "pool":
            pump_ctr[0] += 1
            if pump_ctr[0] % 3 == 0:
                pump()

    def CP(eng, out, in_, reads, writes):
        op(eng, lambda e: e.tensor_copy(out=out, in_=in_), reads, writes)

    def MS(eng, ap, val, writes):
        op(eng, lambda e: e.memset(ap, val), [], writes)

    def SCAN(out, d0, d1, reads, writes):
        op("dve", lambda e: e.tensor_tensor_scan(out=out, data0=d0, data1=d1, initial=0.0, op0=ALU.mult, op1=ALU.add), reads, writes)

    def RECIP(out, in_, reads, writes):
        op("dve", lambda e: e.reciprocal(out=out, in_=in_), reads, writes)

    def DMA(eng, out, in_, reads, writes, dsem):
        op(eng, lambda e: e.dma_start(out=out, in_=in_), reads, writes, dsem=dsem)

    def PS(b):
        return ("ps", b)

    class Rot:
        def __init__(self, n, base=0):
            self.n, self.i, self.base = n, 0, base

        def next(self):
            v = self.base + self.i % self.n
            self.i += 1
            return v

    bank = Rot(8)
    wslot = Rot(3)

    def wview(off_bytes, shape, dtype):
        esz = 4 if dtype == F32 else 2
        n = 1
        for s_ in shape[1:]:
            n *= s_
        assert off_bytes % 4 == 0 and (n * esz) % 4 == 0
        assert off_bytes + n * esz <= WBYTES, (off_bytes, n * esz)
        v = regW[:, off_bytes // 4:(off_bytes + n * esz) // 4]
        if dtype != F32:
            v = v.bitcast(dtype)
        if len(shape) == 3:
            v = v.rearrange("p (a b) -> p a b", a=shape[1])
        return v


    cast_i = [0]

    def cast_w(src, dst, rows, cols, key, defer=False):
        piece = cols
        while piece > 2048:
            piece //= 2
        npc = cols // piece
        R = 256
        for r0 in range(0, rows, R):
            s_ = src[r0:r0 + R, :].rearrange("r (a b) -> r a b", a=npc)
            d_ = dst[r0:r0 + R, :].rearrange("r (a b) -> r a b", a=npc)
            if defer:
                deferred.append((d_, s_, key))
                continue
            DMA("pool", d_, s_, [], [key, ("castslot", cast_i[0] % 3)], "c_" + key)
            cast_i[0] += 1

    cast_w(w_in, win_b, D, P_IN, "win")
    cast_w(rgw, rgw_b, 4096, 256, "rgwb")
    cast_w(w_out, wout_b, D, D, "wout", defer=True)
    cast_w(w_up, wup_b, D, DFF, "wup", defer=True)
    cast_w(w_down, wdown_b, DFF, D, "wdown", defer=True)

    DMA("sp", pp_t[:], pp[:, :], [], ["pp"], "d_pp")
    DMA("sp", cst_t[:], cst[:, :], [], ["cst"], "d_cst")
    gkw_f = wview(0, [128, 512], F32)
    DMA("sp", gkw_f[0:17, :], gkw[:, 0:512], [], ["gkwf"], "d_gk")
    DMA("sp", gkw_f[32:49, :], gkw[:, 512:1024], [], ["gkwf"], "d_gk")
    MS("dve", eps_t[:], EPS, ["eps"])
    MS("dve", one_t[:], 1.0, ["one"])
    MS("dve", mhalf_t[:], -0.5, ["mhalf"])
    MS("dve", ones1[:], 1.0, ["ones1"])
    CP("dve", idb[:], cst_t[:, 0:128], ["cst"], ["idb"])
    CP("dve", gkw_t[0:17, :], gkw_f[0:17, :], ["gkwf"], ["gkw"])
    CP("dve", gkw_t[32:49, :], gkw_f[32:49, :], ["gkwf"], ["gkw"])
    AC(ppx[:], pp_t[:, 104:120], AF.Exp, ["pp"], ["ppx"], scale=-1.0)
    AC(ppx[:], ppx[:], AF.Ln, ["ppx", "one"], ["ppx"], bias=one_t[:, 0:1])
    TS("dve", ppx[:], ppx[:], -8.0, None, ALU.mult, None, ["ppx"], ["ppx"])
    TS("dve", ppx2[:], ppx[:], 2.0, None, ALU.mult, None, ["ppx"], ["ppx2"])
    for (dst, off, n) in [(wpost_b, 0, 2048), (wfpost_b, 2048, 2048), (gnorm_b, 4096, 256)]:
        DMA("sp", dst[:, :], bvec[0:1, off:off + n].broadcast_to([128, n]), [], [dst.name], "d_bv")
    maskT = [cst_t[:, 128:256], cst_t[:, 256:384]]
    Ucum = [cst_t[:, 384:512], cst_t[:, 512:640]]
    Mks = [cst_t[:, 640:768], cst_t[:, 768:896]]

    WKEYS = ["win", "rgwb", "wout", "wup", "wdown"]
    SC.barrier(keep=WKEYS)
    hT = regA[:].rearrange("p (k t) -> p k t", k=16)[:, :, 0:S]

    def hk(k):
        return ("hT", k)

    def rms_stats(src_ap, src_keys, n, col, junk, junk_key):
        cs = sm[:, col:col + 1]
        AC(junk, src_ap, AF.Square, src_keys, [junk_key, ("sm", col)], accum=cs)
        AC(cs, cs, AF.Sqrt, [("sm", col), "eps"], [("sm", col)], bias=eps_t[:, 0:1], scale=1.0 / n)
        RECIP(cs, cs, [("sm", col)], [("sm", col)])

    def rms_stats_pool(src_ap, src_keys, n, col, junk, junk_key):
        cs = sm[:, col:col + 1]
        AC(junk, src_ap, AF.Square, src_keys, [junk_key, ("sm", col)], accum=cs)
        TS("pool", cs, cs, 1.0 / n, EPS, ALU.mult, ALU.add, [("sm", col)], [("sm", col)])
        TT("pool", cs, cs, mhalf_t[:, 0:1], ALU.pow, [("sm", col), "mhalf"], [("sm", col)])

    def transpose_to(src_bf, src_keys, nchunk, dst_fn, scale_fn, dst_keys_fn):
        for g0 in range(0, nchunk, 4):
            b = bank.next()
            pb = ps[:, b, :].bitcast(BF16)
            gn = min(4, nchunk - g0)
            for i in range(gn):
                c = g0 + i
                TR(pb[:, i * 128:(i + 1) * 128], src_bf[:, c * 128:(c + 1) * 128], list(src_keys) + ["idb"], [PS(b)], signal=(i == gn - 1))
            for i in range(gn):
                c = g0 + i
                sc = scale_fn(c)
                src = pb[:, i * 128:(i + 1) * 128]
                if (g0 // 4) % 2 == 0:
                    if sc is None:
                        AC(dst_fn(c), src, AF.Copy, [PS(b)], dst_keys_fn(c))
                    else:
                        AC(dst_fn(c), src, AF.Copy, [PS(b), "pp"], dst_keys_fn(c), scale=sc)
                else:
                    if sc is None:
                        CP("dve", dst_fn(c), src, [PS(b)], dst_keys_fn(c))
                    else:
                        TS("dve", dst_fn(c), src, sc, None, ALU.mult, None, [PS(b), "pp"], dst_keys_fn(c))

    def load_w(src2d, col0, ncols, key_reads, slot=None, sub0=0, rows0=0):
        if slot is None:
            slot = wslot.next()
        src = src2d[rows0:rows0 + 2048, col0:col0 + ncols].rearrange("(k p) c -> p k c", p=128)
        DMA("sp", wbuf[slot][:, :, sub0:sub0 + ncols], src, key_reads, [("wb", slot)], "d_wb%d" % slot)
        return slot

    def proj_fm(slot, sub0, m, evac):
        for blk in range(NB):
            b = bank.next()
            for k in range(16):
                MM(ps[0:m, b, :], wbuf[slot][:, k, sub0:sub0 + m], hT[:, k, blk * 512:(blk + 1) * 512],
                   k == 0, k == 15, [("wb", slot), hk(k)], [PS(b)], signal=(k == 15))
            evac(blk, b)

    for s in range(NSEQ if stop_after != "setup" else 0):
        xt = [wview(i * 8192, [128, 2048], F32) for i in range(2)]
        xs2 = [wview(16384 + i * 4096, [128, 2048], BF16) for i in range(2)]
        jk1 = [wview(24576 + i * 4096, [128, 2048], BF16) for i in range(2)]
        for tt in range(NT):
            r0 = s * S + tt * 128
            xb = tt % 2
            DMA("sp", xt[xb], x[r0:r0 + 128, :], [], [("xt", xb)], "d_xt%d" % xb)
            if stop_after == "p1x":
                continue
            xs, xsk = xs2[xb], ("xs", xb)
            rms_stats(xt[xb], [("xt", xb)], 2048, xb, jk1[xb], ("jk1", xb))
            if stop_after == "p1a":
                continue
            TS("dve", xs, xt[xb], sm[:, xb:xb + 1], None, ALU.mult, None, [("xt", xb), ("sm", xb)], [xsk])
            if stop_after == "p1b":
                continue
            transpose_to(xs, [xsk], 16,
                         lambda c, tt=tt: hT[:, c, tt * 128:(tt + 1) * 128],
                         lambda c: pp_t[:, c:c + 1],
                         lambda c: [hk(c)])
        def rg_loads(n_):
            sx_ = load_w(win_b, n_ * 256, 256, ["win"])
            sg_ = load_w(win_b, 1024 + n_ * 256, 256, ["win"])
            for mi_ in range(4):
                src_ = rgw_b[mi_ * 1024 + n_ * 256: mi_ * 1024 + (n_ + 1) * 256, :].rearrange("(c p) o -> p c o", p=128)
                DMA("sp", rgw_t[:, mi_, :, :], src_, ["rgwb"], ["rgw_t"], "d_rgw")
            return sx_, sg_
        rg_pre = rg_loads(0)
        SC.barrier(keep=WKEYS)

        if stop_after in ("p1", "p1x", "p1a", "p1b"):
            break
        SP4 = (S + 4) * 4
        Bf = [wview(i * SP4, [128, S + 4], F32) for i in range(6)]
        off = 6 * SP4
        XB = [wview(off + i * S * 2, [128, S], BF16) for i in range(2)]
        off += 2 * S * 2
        MO = [wview(off, [128, S], BF16)]
        off += S * 2
        assert off <= WBYTES, off
        mo_i = Rot(1)
        XR = [0, 1]
        XC = [2, 3]
        for n in range(4):
            sx, sg = rg_pre if n == 0 else rg_loads(n)
            for cc in range(2):
                ch = n * 2 + cc
                xr = Bf[XR[cc]]
                kxr = ("B", XR[cc])
                MS("pool", xr[:, 0:2], 0.0, [kxr])
                MS("pool", xr[:, S + 2:S + 4], 0.0, [kxr])

                def ev_x(blk, b, xr=xr, kxr=kxr):
                    dst = xr[:, 2 + blk * 512: 2 + (blk + 1) * 512]
                    if blk % 2 == 0:
                        AC(dst, ps[:, b, :], AF.Copy, [PS(b)], [kxr])
                    else:
                        CP("dve", dst, ps[:, b, :], [PS(b)], [kxr])
                proj_fm(sx, cc * 128, 128, ev_x)
                xc = Bf[XC[cc]]
                kxc = ("B", XC[cc])
                TS("dve", xc[:, 0:S], xr[:, 0:S], pp_t[:, 32 + ch:33 + ch], pp_t[:, 64 + ch:65 + ch], ALU.mult, ALU.add,
                   [kxr, "pp"], [kxc])
                for j in range(1, 4):
                    STT(xc[:, 0:S], xr[:, j:j + S], pp_t[:, 32 + j * 8 + ch:33 + j * 8 + ch], xc[:, 0:S], ALU.mult, ALU.add,
                        [kxr, kxc, "pp"], [kxc])
                CP("pool", XB[cc][:, :], xc[:, 0:S], [kxc], [("XB", cc)])
            for oc in range(2):
                ch = n * 2 + oc
                xc = Bf[XC[oc]]
                kxc = ("B", XC[oc])
                bank_u = Rot(4, 0)
                for d in range(2):
                    for gi in range(2):
                        mi = 2 * d + gi
                        bcol = 72 + 16 * d + 8 * gi + ch
                        gdst = Bf[1] if gi == 0 else Bf[4]
                        gkey = ("B", 1) if gi == 0 else ("B", 4)
                        for blk in range(NB):
                            b = bank_u.next()
                            for cc in range(2):
                                MM(ps[:, b, :], rgw_t[:, mi, cc, oc * 128:(oc + 1) * 128], XB[cc][:, blk * 512:(blk + 1) * 512],
                                   cc == 0, cc == 1, ["rgw_t", ("XB", cc)], [PS(b)], signal=(cc == 1))
                            AC(gdst[:, blk * 512:(blk + 1) * 512], ps[:, b, :], AF.Sigmoid, [PS(b), "pp"], [gkey],
                               bias=pp_t[:, bcol:bcol + 1])
                    if d == 0:
                        for blk in range(NB):
                            bgp = 4 + blk
                            for k in range(16):
                                MM(ps[:, bgp, :], wbuf[sg][:, k, oc * 128:(oc + 1) * 128], hT[:, k, blk * 512:(blk + 1) * 512],
                                   k == 0, k == 15, [("wb", sg), hk(k)], [PS(bgp)], signal=(k == 15))
                    a_, u_, tmp = Bf[1][:, 0:S], Bf[4][:, 0:S], Bf[0][:, 0:S]
                    ka, ku, kt = ("B", 1), ("B", 4), ("B", 0)
                    ccol = 8 * d + ch
                    TT("pool", u_, u_, xc[:, 0:S], ALU.mult, [ku, kxc], [ku])
                    AC(tmp, a_, AF.Exp, [ka, "ppx2"], [kt], scale=ppx2[:, ccol:ccol + 1])
                    AC(a_, a_, AF.Exp, [ka, "ppx"], [ka], scale=ppx[:, ccol:ccol + 1])
                    AC(tmp, tmp, AF.Sqrt, [kt, "one"], [kt], bias=one_t[:, 0:1], scale=-1.0)
                    TT("dve", u_, u_, tmp, ALU.mult, [ku, kt], [ku])
                    if d == 0:
                        SCAN(Bf[5][:, 0:S], a_, u_, [ka, ku], [("B", 5)])
                    else:
                        SCAN(Bf[0][:, 0:S][:, ::-1], a_[:, ::-1], u_[:, ::-1], [ka, ku], [("B", 0)])
                TT("dve", Bf[5][:, 0:S], Bf[5][:, 0:S], Bf[0][:, 0:S], ALU.add, [("B", 5), ("B", 0)], [("B", 5)])
                for blk in range(NB):
                    AC(Bf[1][:, blk * 512:(blk + 1) * 512], ps[:, 4 + blk, :], AF.Gelu, [PS(4 + blk)], [("B", 1)])
                mi_ = mo_i.next()
                TT("dve", MO[mi_][:, :], Bf[5][:, 0:S], Bf[1][:, 0:S], ALU.mult, [("B", 5), ("B", 1)], [("MO", mi_)])
                row0 = (s * 16 + ch) * 128
                DMA("sp", mix_scr[row0:row0 + 128, :], MO[mi_][:, :], [("MO", mi_)], [("mix", s, ch)], "d_mo%d" % mi_)
        def gla_loads(h_):
            sqk_ = load_w(win_b, 2048 + h_ * 128, 128, ["win"])
            load_w(win_b, 2560 + h_ * 128, 128, ["win"], slot=sqk_, sub0=128)
            skv_ = load_w(win_b, 2560 + h_ * 128, 128, ["win"])
            load_w(win_b, 3072 + h_ * 256, 256, ["win"], slot=skv_, sub0=128)
            sgg_ = load_w(win_b, 4096 + h_ * 256, 256, ["win"])
            return sqk_, skv_, sgg_
        DMA("sp", wlr_t[:], win_b[:, 5120:5152].rearrange("(k p) c -> p k c", p=128), ["win"], ["wlr"], "d_wlr")
        gla_pre = gla_loads(0)
        SC.barrier(keep=WKEYS)

        if stop_after == "rg":
            break
        o = [0]

        def carve(shape, dtype):
            n_ = 1
            for s_ in shape[1:]:
                n_ *= s_
            v = wview(o[0], shape, dtype)
            o[0] += ((n_ * (4 if dtype == F32 else 2) + 3) // 4) * 4
            return v
        lrT_all = carve([128, S], BF16)
        qeT = [carve([128, S], BF16) for _ in range(2)]
        keT = [carve([128, S], BF16) for _ in range(2)]
        ks = [carve([128, NT, 128], BF16) for _ in range(2)]
        v_sb = carve([128, NT, 256], BF16)
        o_acc = carve([128, NT, 256], F32)
        dec = [carve([128, NT], F32) for _ in range(2)]
        lpair = carve([128, 256], F32)
        eqpair = carve([128, 256], F32)
        l_t = [lpair[:, d * 128:(d + 1) * 128] for d in range(2)]
        eq_t = [eqpair[:, d * 128:(d + 1) * 128] for d in range(2)]
        ek_t = [carve([128, 128], F32) for _ in range(2)]
        gs2 = [lpair, eqpair]
        gs2k = [[("l", 0), ("l", 1)], [("eq", 0), ("eq", 1)]]
        gob2 = [ek_t[i].bitcast(BF16) for i in range(2)]
        ksf_t = [carve([128, 128], F32) for _ in range(2)]
        at_t = [carve([128, 128], BF16) for _ in range(2)]
        S32 = [carve([128, 256], F32) for _ in range(2)]
        Sbf = [[carve([128, 256], BF16) for _ in range(2)] for _ in range(2)]
        assert o[0] <= WBYTES, o[0]

        for d in range(2):
            MS("pool", lrT_all[32 * d:32 * d + 32, :], 1.0, [("lrT", d)])
            for blk in range(NB):
                b = bank.next()
                for k in range(16):
                    MM(ps[0:16, b, :], wlr_t[:, k, d * 16:(d + 1) * 16], hT[:, k, blk * 512:(blk + 1) * 512],
                       k == 0, k == 15, ["wlr", hk(k)], [PS(b)], signal=(k == 15))
                AC(lrT_all[32 * d:32 * d + 16, blk * 512:(blk + 1) * 512], ps[0:16, b, :], AF.Copy, [PS(b)], [("lrT", d)])

        for h in range(4):
            sqk, skv, sgg = gla_pre if h == 0 else gla_loads(h)
            def kv_tok(c_):
                bt_ = 4 + c_ % 2
                tsl_ = slice(c_ * 128, (c_ + 1) * 128)
                for k in range(16):
                    MM(ps[:, bt_, 0:384], hT[:, k, tsl_], wbuf[skv][:, k, 0:384], k == 0, k == 15,
                       [("wb", skv), hk(k)], [PS(bt_)], signal=(k == 15))
            for blk in range(NB):
                bq, bk = (0, 1) if blk % 2 == 0 else (2, 3)
                for (bb, sub) in ((bq, 0), (bk, 128)):
                    for k in range(16):
                        MM(ps[:, bb, :], wbuf[sqk][:, k, sub:sub + 128], hT[:, k, blk * 512:(blk + 1) * 512],
                           k == 0, k == 15, [("wb", sqk), hk(k)], [PS(bb)], signal=(k == 15))
                if blk == 0:
                    kv_tok(0)
                for ti in range(4):
                    c = blk * 4 + ti
                    tsl = slice(c * 128, (c + 1) * 128)
                    psl = slice(ti * 128, (ti + 1) * 128)
                    bt = 4 + c % 2
                    bxc = [6, 7]
                    for d in range(2):
                        MM(ps[:, bxc[d], 256:384], lrT_all[32 * d:32 * d + 17, tsl], gkw_t[32 * d:32 * d + 17, h * 128:(h + 1) * 128],
                           True, True, [("lrT", d), "gkw"], [PS(bxc[d])])
                    if c + 1 < NT:
                        kv_tok(c + 1)
                    for d in range(2):
                        AC(l_t[d][:, :], ps[:, bxc[d], 256:384], AF.Exp, [PS(bxc[d])], [("l", d)], scale=-1.0)
                    for d in range(2):
                        AC(l_t[d][:, :], l_t[d][:, :], AF.Ln, [("l", d), "one"], [("l", d)], bias=one_t[:, 0:1])
                    for d in range(2):
                        bc = bxc[d]
                        MM(ps[:, bc, 0:128], l_t[d][:, :], Ucum[d], True, True, [("l", d), "cst"], [PS(bc)], signal=False)
                        MM(ps[:, bc, 128:256], Mks[d], l_t[d][:, :], True, True, [("l", d), "cst"], [PS(bc)])
                    AC(v_sb[:, c, :], ps[:, bt, 128:384], AF.Copy, [PS(bt)], [("v", c)])
                    for d in range(2):
                        bc = bxc[d]
                        AC(eq_t[d][:, :], ps[:, bc, 0:128], AF.Exp, [PS(bc)], [("eq", d)])
                        AC(ek_t[d][:, :], ps[:, bc, 0:128], AF.Exp, [PS(bc)], [("ek", d)], scale=-1.0)
                        lastc = 127 if d == 0 else 0
                        AC(dec[d][:, c:c + 1], ps[:, bc, lastc:lastc + 1], AF.Exp, [PS(bc)], [("dec", d)])
                        AC(ksf_t[d][:, :], ps[:, bc, 128:256], AF.Exp, [PS(bc)], [("ksf", d)])
                    for d in range(2):
                        STT(qeT[d][:, tsl], ps[:, bq, psl], 128.0 ** -0.5, eq_t[d][:, :], ALU.mult, ALU.mult,
                            [PS(bq), ("eq", d)], [("qe", d, c)])
                        TT("dve", keT[d][:, tsl], ps[:, bk, psl], ek_t[d][:, :], ALU.mult, [PS(bk), ("ek", d)], [("ke", d, c)])
                        TT("dve", ks[d][:, c, :], ps[:, bt, 0:128], ksf_t[d][:, :], ALU.mult, [PS(bt), ("ksf", d)], [("ks", d, c)])
            touched = set()
            for i in range(NT):
                for d in range(2):
                    c = i if d == 0 else NT - 1 - i
                    tsl = slice(c * 128, (c + 1) * 128)
                    first = (i == 0)
                    if i < NT - 1:
                        bkv = bank.next()
                        MM(ps[:, bkv, 0:256], ks[d][:, c, :], v_sb[:, c, :], True, True, [("ks", d, c), ("v", c)], [PS(bkv)])
                        if first:
                            CP("dve", S32[d][:, :], ps[:, bkv, 0:256], [PS(bkv)], [("S32", d)])
                        else:
                            STT(S32[d][:, :], S32[d][:, :], dec[d][:, c:c + 1], ps[:, bkv, 0:256], ALU.mult, ALU.add,
                                [PS(bkv), ("S32", d), ("dec", d)], [("S32", d)])
                        CP("dve", Sbf[d][i % 2][:, :], S32[d][:, :], [("S32", d)], [("Sbf", d, i % 2)])
                    ba = bank.next()
                    MM(ps[:, ba, 0:128], keT[d][:, tsl], qeT[d][:, tsl], True, True, [("ke", d, c), ("qe", d, c)], [PS(ba)])
                    TT("dve", at_t[d][:, :], ps[:, ba, 0:128], maskT[d], ALU.mult, [PS(ba), "cst"], [("at", d)])
                    bo = bank.next()
                    MM(ps[:, bo, 0:256], at_t[d][:, :], v_sb[:, c, :], True, first, [("at", d), ("v", c)], [PS(bo)], signal=first)
                    if not first:
                        MM(ps[:, bo, 0:256], qeT[d][:, tsl], Sbf[d][(i - 1) % 2][:, :], False, True,
                           [("qe", d, c), ("Sbf", d, (i - 1) % 2)], [PS(bo)])
                    if c not in touched:
                        touched.add(c)
                        AC(o_acc[:, c, :], ps[:, bo, 0:256], AF.Copy, [PS(bo)], [("oa", c)])
                    else:
                        TT("dve", o_acc[:, c, :], ps[:, bo, 0:256], o_acc[:, c, :], ALU.add, [PS(bo), ("oa", c)], [("oa", c)])
            def g_mm(c_):
                tsl_ = slice(c_ * 128, (c_ + 1) * 128)
                bg_ = bank.next()
                for k in range(16):
                    MM(ps[:, bg_, 0:256], hT[:, k, tsl_], wbuf[sgg][:, k, 0:256], k == 0, k == 15,
                       [("wb", sgg), hk(k)], [PS(bg_)], signal=(k == 15))
                return bg_
            bg_next = g_mm(0)
            for c in range(NT):
                tsl = slice(c * 128, (c + 1) * 128)
                bg = bg_next
                pi = c % 2
                gs_t, gsk, gob, gobk = gs2[pi], gs2k[pi], gob2[pi], ("ek", pi)
                smc = 8 + pi
                AC(gs_t[:, :], ps[:, bg, 0:256], AF.Silu, [PS(bg)], gsk)
                rms_stats_pool(o_acc[:, c, :], [("oa", c)], 256, smc, gob[:, :], gobk)
                STT(o_acc[:, c, :], o_acc[:, c, :], sm[:, smc:smc + 1], gnorm_b[:, :], ALU.mult, ALU.mult,
                    [("oa", c), ("sm", smc), "gnorm_b"], [("oa", c)])
                TT("pool", gob[:, :], o_acc[:, c, :], gs_t[:, :], ALU.mult, [("oa", c)] + gsk, [gobk])
                if c + 1 < NT:
                    bg_next = g_mm(c + 1)
                transpose_to(gob, [gobk], 2, lambda cc, tsl=tsl: qeT[cc][:, tsl], lambda cc: None,
                             lambda cc, c=c: [("qe", cc, c)])
            for cc in range(2):
                ch = 8 + h * 2 + cc
                row0 = (s * 16 + ch) * 128
                DMA("sp", mix_scr[row0:row0 + 128, :], qeT[cc][:, :], [("qe", cc, c) for c in range(NT)], [("mix", s, ch)], "d_mog%d" % cc)
        pre_wout = [load_w(wout_b, nb * 512, 512, ["wout"]) for nb in range(2)] if s > 0 else None
        SC.barrier(keep=WKEYS)

        pump(1000)
        if stop_after == "gla":
            break
        mixT = regA[:, 0:8192].rearrange("p (k t) -> p k t", k=16)
        hfT = regA[:, 8192:16384].rearrange("p (k t) -> p k t", k=16)
        aT = regA[:, 16384:24576].rearrange("p (k t) -> p k t", k=16)
        hf2 = [regA[:, 24576 + i * 2048:24576 + (i + 1) * 2048] for i in range(2)]
        rl_t = [regA[:, 28672 + i * 1024: 28672 + (i + 1) * 1024].bitcast(F32) for i in range(2)]

        def junk_of(ti):
            return regA[:, 16384 + ti * 2048:16384 + (ti + 1) * 2048], [("aT", 4 * ti + i) for i in range(4)]

        def load_mixT(blk_):
            src_ = mix_scr[s * 2048:(s + 1) * 2048, blk_ * 512:(blk_ + 1) * 512].rearrange("(k p) t -> p k t", p=128)
            DMA("sp", mixT, src_, [("mix", s, ch) for ch in range(16)], ["mixT"], "d_mixT")
        load_mixT(0)
        x1 = wview(0, [128, 4, 2048], F32)
        ft = wview(32768, [128, 4, 2048], F32)
        rl_i = Rot(2)
        if pre_wout is None:
            pre_wout = [load_w(wout_b, nb * 512, 512, ["wout"]) for nb in range(2)]
        for blk in range(NB):
            t0 = s * S + blk * 512
            for ti in range(4):
                DMA("pool", x1[:, ti, :], x[t0 + ti * 128: t0 + (ti + 1) * 128, :], [], [("x1", ti)], "d_x1%d" % ti)
            def s1(ti):
                jk, jkk = junk_of(ti)
                c1 = 16 + ti
                rms_stats(ft[:, ti, :], [("ft", ti)], 2048, c1, jk, jkk[0])
                STT(ft[:, ti, :], ft[:, ti, :], sm[:, c1:c1 + 1], wpost_b[:, :], ALU.mult, ALU.mult,
                    [("ft", ti), ("sm", c1), "wpost_b"] + jkk[1:], [("ft", ti)])
                TT("pool" if ti % 2 == 0 else "dve", x1[:, ti, :], x1[:, ti, :], ft[:, ti, :], ALU.add,
                   [("x1", ti), ("ft", ti)], [("x1", ti)])

            def s2a(ti):
                jk, jkk = junk_of(ti)
                c2 = 20 + ti
                rms_stats(x1[:, ti, :], [("x1", ti)], 2048, c2, jk, jkk[0])
                TS("dve", hf2[ti % 2], x1[:, ti, :], sm[:, c2:c2 + 1], None, ALU.mult, None,
                   [("x1", ti), ("sm", c2)], [("hf_t", ti % 2)])

            def s2b(ti):
                transpose_to(hf2[ti % 2], [("hf_t", ti % 2)], 16,
                             lambda c, ti=ti: hfT[:, c, ti * 128:(ti + 1) * 128],
                             lambda c: pp_t[:, 16 + c:17 + c],
                             lambda c: [("hfT", c)])
            for nb in range(4):
                sl = pre_wout[nb] if nb < 2 else load_w(wout_b, nb * 512, 512, ["wout"])
                for ti in range(4):
                    b = bank.next()
                    for k in range(16):
                        MM(ps[:, b, :], mixT[:, k, ti * 128:(ti + 1) * 128], wbuf[sl][:, k, :], k == 0, k == 15,
                           [("wb", sl), "mixT"], [PS(b)], signal=(k == 15))
                    dst = ft[:, ti, nb * 512:(nb + 1) * 512]
                    if (nb + ti) % 2 == 0:
                        AC(dst, ps[:, b, :], AF.Copy, [PS(b)], [("ft", ti)])
                    else:
                        CP("dve", dst, ps[:, b, :], [PS(b)], [("ft", ti)])
                    if nb == 3:
                        s1(ti)
                        if ti in (1, 2):
                            s2a(ti - 1)
            s2b(0)
            s2b(1)
            s2a(2)
            s2b(2)
            s2a(3)
            s2b(3)
            for q in range(4):
                for jg in range(4):
                    sl = load_w(wup_b, (q * 16 + jg * 4) * 128, 512, ["wup"])
                    if q == 0 and jg == 2 and blk + 1 < NB:
                        load_mixT(blk + 1)
                    for jj in range(4):
                        j = jg * 4 + jj
                        b = bank.next()
                        for k in range(16):
                            MM(ps[:, b, :], wbuf[sl][:, k, jj * 128:(jj + 1) * 128], hfT[:, k, :], k == 0, k == 15,
                               [("wb", sl), ("hfT", k)], [PS(b)], signal=(k == 15))
                        ri = rl_i.next()
                        AC(rl_t[ri], ps[:, b, :], AF.Relu, [PS(b)], [("rl", ri)])
                        TT("pool" if j % 2 == 0 else "dve", aT[:, j, :], rl_t[ri], rl_t[ri], ALU.mult, [("rl", ri)], [("aT", j)])
                for nb in range(4):
                    sl = load_w(wdown_b, nb * 512, 512, ["wdown"], rows0=q * 2048)
                    for ti in range(4):
                        b = bank.next()
                        for j in range(16):
                            MM(ps[:, b, :], aT[:, j, ti * 128:(ti + 1) * 128], wbuf[sl][:, j, :], j == 0, j == 15,
                               [("wb", sl), ("aT", j)], [PS(b)], signal=(j == 15))
                        fsl = ft[:, ti, nb * 512:(nb + 1) * 512]
                        if q == 0:
                            AC(fsl, ps[:, b, :], AF.Copy, [PS(b)], [("ft", ti)])
                        else:
                            TT("dve", fsl, ps[:, b, :], fsl, ALU.add, [PS(b), ("ft", ti)], [("ft", ti)])
            if blk + 1 < NB:
                pre_wout = [load_w(wout_b, nb * 512, 512, ["wout"]) for nb in range(2)]
            for ti in range(4):
                jk, jkk = junk_of(ti)
                c3 = 24 + ti
                rms_stats(ft[:, ti, :], [("ft", ti)], 2048, c3, jk, jkk[0])
                STT(ft[:, ti, :], ft[:, ti, :], sm[:, c3:c3 + 1], wfpost_b[:, :], ALU.mult, ALU.mult,
                    [("ft", ti), ("sm", c3), "wfpost_b"] + jkk[1:], [("ft", ti)])
                TT("dve", ft[:, ti, :], ft[:, ti, :], x1[:, ti, :], ALU.add, [("x1", ti), ("ft", ti)], [("ft", ti)])
                DMA("pool", y[t0 + ti * 128: t0 + (ti + 1) * 128, :], ft[:, ti, :], [("ft", ti)], [("y", t0, ti)], "d_y%d" % ti)
        SC.barrier(keep=WKEYS)

    with nc.Block() as block:
        @block.tensor
        def _(e):
            SC.replay("pe", e)

        @block.scalar
        def _(e):
            SC.replay("act", e)

        @block.vector
        def _(e):
            SC.replay("dve", e)

        @block.gpsimd
        def _(e):
            SC.replay("pool", e)

        @block.sync
        def _(e):
            SC.replay("sp", e)
    return nc


def host_prep(inp):
    f = np.float32

    def colpack(v):
        v = np.asarray(v, f).reshape(-1, 128)
        return np.ascontiguousarray(v.T)
    cols = [colpack(inp["norm_mix_pre"][0]), colpack(inp["norm_ffn_pre"][0])]
    cw = np.asarray(inp["rg_conv_w"][0], f)
    for j in range(4):
        cols.append(colpack(cw[j]))
    cols.append(colpack(inp["rg_conv_b"][0]))
    for nm in ["rg_a_b_fwd", "rg_i_b_fwd", "rg_a_b_bwd", "rg_i_b_bwd", "rg_lambda_fwd", "rg_lambda_bwd"]:
        cols.append(colpack(inp[nm][0]))
    pp = np.ascontiguousarray(np.concatenate(cols, axis=1))
    assert pp.shape == (128, 120)
    bvec = np.concatenate([np.asarray(inp["norm_mix_post"][0], f), np.asarray(inp["norm_ffn_post"][0], f),
                           np.asarray(inp["gla_norm"][0], f)])[None, :]
    gkw = np.zeros((17, 1024), f)
    gkw[:16, :512] = inp["gla_gk_w_fwd"][0]
    gkw[16, :512] = inp["gla_gk_b_fwd"][0]
    gkw[:16, 512:] = inp["gla_gk_w_bwd"][0]
    gkw[16, 512:] = inp["gla_gk_b_bwd"][0]
    rgw = np.concatenate([np.asarray(inp[nm][0], f).reshape(1024, 256)
                          for nm in ["rg_a_w_fwd", "rg_i_w_fwd", "rg_a_w_bwd", "rg_i_w_bwd"]], axis=0)
    i_ = np.arange(128)
    tp, t = i_[:, None], i_[None, :]
    g = -1.0 / 16.0
    cst = np.concatenate([
        np.eye(128, dtype=f),
        (tp <= t).astype(f),
        (tp >= t).astype(f),
        (tp <= t).astype(f) * g,
        (tp >= t).astype(f) * g,
        (tp > t).astype(f) * g,
        (tp < t).astype(f) * g,
    ], axis=1).astype(f)
    return dict(
        w_in=np.ascontiguousarray(inp["w_in"][0], f), w_out=np.ascontiguousarray(inp["w_out"][0], f),
        w_up=np.ascontiguousarray(inp["w_up"][0], f), w_down=np.ascontiguousarray(inp["w_down"][0], f),
        rgw=np.ascontiguousarray(rgw), pp=pp, bvec=np.ascontiguousarray(bvec), gkw=gkw, cst=np.ascontiguousarray(cst))


_NC_CACHE = {}


def kernel(**inputs):
    S = 2048
    NSEQ = 3
    xp = np.asarray(inputs["x_prompt"], np.float32)
    xs = np.asarray(inputs["x_sample"], np.float32)
    allx = np.concatenate([xp, xs], axis=0)
    shared = host_prep(inputs)
    key = (S, NSEQ)
    if key not in _NC_CACHE:
        _NC_CACHE[key] = build(S, NSEQ)
    nc = _NC_CACHE[key]
    in_maps = []
    for c in range(NCORE):
        m = dict(shared)
        m["x"] = np.ascontiguousarray(allx[c * NSEQ:(c + 1) * NSEQ].reshape(NSEQ * S, D))
        in_maps.append(m)
    res = run_bass_kernel_spmd(nc, in_maps, core_ids=list(range(NCORE)))
    ys = np.concatenate([np.asarray(r["y"], np.float32).reshape(NSEQ, S, D) for r in res.results], axis=0)
    return ys[:8].copy(), ys[8:].copy()
```

```python
import numpy as np
import concourse.bass as bass
import concourse.mybir as mybir
from concourse.bass_utils import run_bass_kernel_spmd

F32 = mybir.dt.float32
BF16 = mybir.dt.bfloat16
AF = mybir.ActivationFunctionType
ALU = mybir.AluOpType

D = 2048
P_IN = 5152
DFF = 8192
NCORE = 8
EPS = 1e-6
ENG = ["pe", "act", "dve", "pool", "sp"]
import os
_EV = os.environ.get("EVSEL", "alt")


def EVAC_SEL(c):
    return {"alt": c % 2 == 0, "act": True, "dve": False}[_EV]


class Sched:
    def __init__(self, nc):
        self.nc = nc
        self.prog = {e: [] for e in ENG}
        self.semh = {}
        self.cnt = {}
        self.lastw = {}
        self.readers = {}
        self.known = {e: {} for e in ENG}
        for e in ["pe", "act", "dve", "pool"]:
            self._mk(e)

    def _mk(self, name):
        if name not in self.semh:
            self.semh[name] = self.nc.alloc_semaphore(name="s_" + name)
            self.cnt[name] = 0

    def op(self, eng, fn, reads=(), writes=(), signal=True, dsem=None):
        pr = [k for k in reads if isinstance(k, tuple) and k[0] == "ps"]
        if pr:
            reads = [k for k in reads if k not in pr]
            writes = list(writes) + pr
        deps = {}

        def add(tok):
            if tok is not None:
                deps[tok[0]] = max(deps.get(tok[0], 0), tok[1])

        for k in reads:
            add(self.lastw.get(k))
        for k in writes:
            add(self.lastw.get(k))
            for sn, v in self.readers.get(k, {}).items():
                add((sn, v))
        waits = []
        for sn, v in deps.items():
            if sn == "pe" and eng == "pe":
                continue
            if self.known[eng].get(sn, 0) >= v:
                continue
            self.known[eng][sn] = v
            waits.append((sn, v))
        if dsem is not None:
            self._mk(dsem)
            self.cnt[dsem] += 16
            tok = (dsem, self.cnt[dsem])
            rec = (waits, fn, dsem, 16)
        elif signal:
            self.cnt[eng] += 1
            tok = (eng, self.cnt[eng])
            rec = (waits, fn, eng, 1)
        else:
            tok = (eng, self.cnt[eng] + 1)
            rec = (waits, fn, None, 0)
        self.prog[eng].append(rec)
        for k in reads:
            r = self.readers.setdefault(k, {})
            r[tok[0]] = max(r.get(tok[0], 0), tok[1])
        for k in writes:
            self.lastw[k] = tok
            self.readers[k] = {}

    def barrier(self, keep=()):
        kept = {k: self.lastw[k] for k in keep if k in self.lastw}
        skip = {tok[0] for tok in kept.values()}
        for e in ENG:
            waits = []
            for sn, c in self.cnt.items():
                if c == 0 or (sn == "pe" and e == "pe") or sn in skip:
                    continue
                if self.known[e].get(sn, 0) >= c:
                    continue
                self.known[e][sn] = c
                waits.append((sn, c))
            if waits:
                self.prog[e].append((waits, None, None, 0))
        self.lastw = dict(kept)
        self.readers = {}

    def replay(self, eng, e):
        for waits, fn, sn, inc in self.prog[eng]:
            for wsn, v in waits:
                e.wait_ge(self.semh[wsn], v)
            if fn is None:
                continue
            ins = fn(e)
            if sn is not None:
                ins.then_inc(self.semh[sn], inc)


def build(S, NSEQ, stop_after=None):
    nc = bass.Bass("TRN2", target_bir_lowering=False)
    NT = S // 128
    NB = S // 512
    T = NSEQ * S
    dt = nc.dram_tensor
    x = dt("x", [T, D], F32, kind="ExternalInput").ap()
    w_in = dt("w_in", [D, P_IN], F32, kind="ExternalInput").ap()
    w_out = dt("w_out", [D, D], F32, kind="ExternalInput").ap()
    w_up = dt("w_up", [D, DFF], F32, kind="ExternalInput").ap()
    w_down = dt("w_down", [DFF, D], F32, kind="ExternalInput").ap()
    rgw = dt("rgw", [4096, 256], F32, kind="ExternalInput").ap()
    pp = dt("pp", [128, 120], F32, kind="ExternalInput").ap()
    bvec = dt("bvec", [1, 4352], F32, kind="ExternalInput").ap()
    gkw = dt("gkw", [17, 1024], F32, kind="ExternalInput").ap()
    cst = dt("cst", [128, 7 * 128], F32, kind="ExternalInput").ap()
    y = dt("y", [T, D], F32, kind="ExternalOutput").ap()
    win_b = dt("win_b", [D, P_IN], BF16, kind="Internal").ap()
    wout_b = dt("wout_b", [D, D], BF16, kind="Internal").ap()
    wup_b = dt("wup_b", [D, DFF], BF16, kind="Internal").ap()
    wdown_b = dt("wdown_b", [DFF, D], BF16, kind="Internal").ap()
    rgw_b = dt("rgw_b", [4096, 256], BF16, kind="Internal").ap()
    mix_scr = dt("mix_scr", [NSEQ * 16 * 128, S], BF16, kind="Internal").ap()

    sb = nc.alloc_sbuf_tensor
    pp_t = sb("pp_t", [128, 120], F32)
    ppx = sb("ppx", [128, 16], F32)
    ppx2 = sb("ppx2", [128, 16], F32)
    cst_t = sb("cst_t", [128, 7 * 128], F32)
    idb = sb("idb", [128, 128], BF16)
    bv_t = sb("bv_t", [1, 512], F32)
    ones1 = sb("ones1", [1, 128], F32)
    wpost_b = sb("wpost_b", [128, 2048], F32)
    wfpost_b = sb("wfpost_b", [128, 2048], F32)
    gnorm_b = sb("gnorm_b", [128, 256], F32)
    gkw_t = sb("gkw_t", [64, 512], BF16)
    eps_t = sb("eps_t", [128, 1], F32)
    one_t = sb("one_t", [128, 1], F32)
    mhalf_t = sb("mhalf_t", [128, 1], F32)
    sm = sb("sm", [128, 32], F32)
    wbuf = [sb("wbuf%d" % i, [128, 16, 512], BF16) for i in range(3)]
    rgw_t = sb("rgw_t", [128, 4, 2, 256], BF16)
    wlr_t = sb("wlr_t", [128, 16, 32], BF16)
    regA = sb("regA", [128, 16 * 2048], BF16)
    WBYTES = 64 * 1024
    regW = sb("regW", [128, WBYTES // 4], F32)
    ps = nc.alloc_psum_tensor("ps", [128, 8, 512], F32)

    SC = Sched(nc)
    op = SC.op

    def MM(out, lhsT, rhs, start, stop, reads, writes, signal=True):
        op("pe", lambda e: e.matmul(out, lhsT, rhs, start=start, stop=stop), reads, writes, signal)

    def TR(out, in_, reads, writes, signal=True):
        op("pe", lambda e: e.transpose(out, in_, idb[:]), reads, writes, signal)

    def AC(out, in_, func, reads, writes, bias=None, scale=1.0, accum=None):
        kw = {}
        if bias is not None:
            kw["bias"] = bias
        if accum is not None:
            kw["accum_out"] = accum
        op("act", lambda e: e.activation(out=out, in_=in_, func=func, scale=scale, **kw), reads, writes)

    def TS(eng, out, in0, s1, s2, op0, op1, reads, writes):
        if op1 is None:
            op(eng, lambda e: e.tensor_scalar(out=out, in0=in0, scalar1=s1, scalar2=None, op0=op0), reads, writes)
        else:
            op(eng, lambda e: e.tensor_scalar(out=out, in0=in0, scalar1=s1, scalar2=s2, op0=op0, op1=op1), reads, writes)

    def STT(out, in0, scalar, in1, op0, op1, reads, writes):
        op("dve", lambda e: e.scalar_tensor_tensor(out=out, in0=in0, scalar=scalar, in1=in1, op0=op0, op1=op1), reads, writes)

    deferred = []
    pump_ctr = [0]

    def pump(n=1):
        for _ in range(n):
            if deferred:
                d_, s_, key = deferred.pop(0)
                DMA("pool", d_, s_, [], [key, ("castslot", cast_i[0] % 3)], "c_" + key)
                cast_i[0] += 1

    def TT(eng, out, in0, in1, aop, reads, writes):
        op(eng, lambda e: e.tensor_tensor(out=out, in0=in0, in1=in1, op=aop), reads, writes)
        if eng == "pool":
            pump_ctr[0] += 1
            if pump_ctr[0] % 3 == 0:
                pump()

    def CP(eng, out, in_, reads, writes):
        op(eng, lambda e: e.tensor_copy(out=out, in_=in_), reads, writes)

    def MS(eng, ap, val, writes):
        op(eng, lambda e: e.memset(ap, val), [], writes)

    def SCAN(out, d0, d1, reads, writes):
        op("dve", lambda e: e.tensor_tensor_scan(out=out, data0=d0, data1=d1, initial=0.0, op0=ALU.mult, op1=ALU.add), reads, writes)

    def RECIP(out, in_, reads, writes):
        op("dve", lambda e: e.reciprocal(out=out, in_=in_), reads, writes)

    def DMA(eng, out, in_, reads, writes, dsem):
        op(eng, lambda e: e.dma_start(out=out, in_=in_), reads, writes, dsem=dsem)

    def PS(b):
        return ("ps", b)

    class Rot:
        def __init__(self, n, base=0):
            self.n, self.i, self.base = n, 0, base

        def next(self):
            v = self.base + self.i % self.n
            self.i += 1
            return v

    bank = Rot(8)
    wslot = Rot(3)

    def wview(off_bytes, shape, dtype):
        esz = 4 if dtype == F32 else 2
        n = 1
        for s_ in shape[1:]:
            n *= s_
        assert off_bytes % 4 == 0 and (n * esz) % 4 == 0
        assert off_bytes + n * esz <= WBYTES, (off_bytes, n * esz)
        v = regW[:, off_bytes // 4:(off_bytes + n * esz) // 4]
        if dtype != F32:
            v = v.bitcast(dtype)
        if len(shape) == 3:
            v = v.rearrange("p (a b) -> p a b", a=shape[1])
        return v


    cast_i = [0]

    def cast_w(src, dst, rows, cols, key, defer=False):
        piece = cols
        while piece > 2048:
            piece //= 2
        npc = cols // piece
        R = 256
        for r0 in range(0, rows, R):
            s_ = src[r0:r0 + R, :].rearrange("r (a b) -> r a b", a=npc)
            d_ = dst[r0:r0 + R, :].rearrange("r (a b) -> r a b", a=npc)
            if defer:
                deferred.append((d_, s_, key))
                continue
            DMA("pool", d_, s_, [], [key, ("castslot", cast_i[0] % 3)], "c_" + key)
            cast_i[0] += 1

    cast_w(w_in, win_b, D, P_IN, "win")
    cast_w(rgw, rgw_b, 4096, 256, "rgwb")
    cast_w(w_out, wout_b, D, D, "wout", defer=True)
    cast_w(w_up, wup_b, D, DFF, "wup", defer=True)
    cast_w(w_down, wdown_b, DFF, D, "wdown", defer=True)

    DMA("sp", pp_t[:], pp[:, :], [], ["pp"], "d_pp")
    DMA("sp", cst_t[:], cst[:, :], [], ["cst"], "d_cst")
    gkw_f = wview(0, [128, 512], F32)
    DMA("sp", gkw_f[0:17, :], gkw[:, 0:512], [], ["gkwf"], "d_gk")
    DMA("sp", gkw_f[32:49, :], gkw[:, 512:1024], [], ["gkwf"], "d_gk")
    MS("dve", eps_t[:], EPS, ["eps"])
    MS("dve", one_t[:], 1.0, ["one"])
    MS("dve", mhalf_t[:], -0.5, ["mhalf"])
    MS("dve", ones1[:], 1.0, ["ones1"])
    CP("dve", idb[:], cst_t[:, 0:128], ["cst"], ["idb"])
    CP("dve", gkw_t[0:17, :], gkw_f[0:17, :], ["gkwf"], ["gkw"])
    CP("dve", gkw_t[32:49, :], gkw_f[32:49, :], ["gkwf"], ["gkw"])
    AC(ppx[:], pp_t[:, 104:120], AF.Exp, ["pp"], ["ppx"], scale=-1.0)
    AC(ppx[:], ppx[:], AF.Ln, ["ppx", "one"], ["ppx"], bias=one_t[:, 0:1])
    TS("dve", ppx[:], ppx[:], -8.0, None, ALU.mult, None, ["ppx"], ["ppx"])
    TS("dve", ppx2[:], ppx[:], 2.0, None, ALU.mult, None, ["ppx"], ["ppx2"])
    for (dst, off, n) in [(wpost_b, 0, 2048), (wfpost_b, 2048, 2048), (gnorm_b, 4096, 256)]:
        DMA("sp", dst[:, :], bvec[0:1, off:off + n].broadcast_to([128, n]), [], [dst.name], "d_bv")
    maskT = [cst_t[:, 128:256], cst_t[:, 256:384]]
    Ucum = [cst_t[:, 384:512], cst_t[:, 512:640]]
    Mks = [cst_t[:, 640:768], cst_t[:, 768:896]]

    WKEYS = ["win", "rgwb", "wout", "wup", "wdown"]
    SC.barrier(keep=WKEYS)
    hT = regA[:].rearrange("p (k t) -> p k t", k=16)[:, :, 0:S]

    def hk(k):
        return ("hT", k)

    def rms_stats(src_ap, src_keys, n, col, junk, junk_key):
        cs = sm[:, col:col + 1]
        AC(junk, src_ap, AF.Square, src_keys, [junk_key, ("sm", col)], accum=cs)
        AC(cs, cs, AF.Sqrt, [("sm", col), "eps"], [("sm", col)], bias=eps_t[:, 0:1], scale=1.0 / n)
        RECIP(cs, cs, [("sm", col)], [("sm", col)])

    def rms_stats_pool(src_ap, src_keys, n, col, junk, junk_key):
        cs = sm[:, col:col + 1]
        AC(junk, src_ap, AF.Square, src_keys, [junk_key, ("sm", col)], accum=cs)
        TS("pool", cs, cs, 1.0 / n, EPS, ALU.mult, ALU.add, [("sm", col)], [("sm", col)])
        TT("pool", cs, cs, mhalf_t[:, 0:1], ALU.pow, [("sm", col), "mhalf"], [("sm", col)])

    def transpose_to(src_bf, src_keys, nchunk, dst_fn, scale_fn, dst_keys_fn):
        for g0 in range(0, nchunk, 4):
            b = bank.next()
            pb = ps[:, b, :].bitcast(BF16)
            gn = min(4, nchunk - g0)
            for i in range(gn):
                c = g0 + i
                TR(pb[:, i * 128:(i + 1) * 128], src_bf[:, c * 128:(c + 1) * 128], list(src_keys) + ["idb"], [PS(b)], signal=(i == gn - 1))
            for i in range(gn):
                c = g0 + i
                sc = scale_fn(c)
                src = pb[:, i * 128:(i + 1) * 128]
                if (g0 // 4) % 2 == 0:
                    if sc is None:
                        AC(dst_fn(c), src, AF.Copy, [PS(b)], dst_keys_fn(c))
                    else:
                        AC(dst_fn(c), src, AF.Copy, [PS(b), "pp"], dst_keys_fn(c), scale=sc)
                else:
                    if sc is None:
                        CP("dve", dst_fn(c), src, [PS(b)], dst_keys_fn(c))
                    else:
                        TS("dve", dst_fn(c), src, sc, None, ALU.mult, None, [PS(b), "pp"], dst_keys_fn(c))

    def load_w(src2d, col0, ncols, key_reads, slot=None, sub0=0, rows0=0):
        if slot is None:
            slot = wslot.next()
        src = src2d[rows0:rows0 + 2048, col0:col0 + ncols].rearrange("(k p) c -> p k c", p=128)
        DMA("sp", wbuf[slot][:, :, sub0:sub0 + ncols], src, key_reads, [("wb", slot)], "d_wb%d" % slot)
        return slot

    def proj_fm(slot, sub0, m, evac):
        for blk in range(NB):
            b = bank.next()
            for k in range(16):
                MM(ps[0:m, b, :], wbuf[slot][:, k, sub0:sub0 + m], hT[:, k, blk * 512:(blk + 1) * 512],
                   k == 0, k == 15, [("wb", slot), hk(k)], [PS(b)], signal=(k == 15))
            evac(blk, b)

    for s in range(NSEQ if stop_after != "setup" else 0):
        xt = [wview(i * 8192, [128, 2048], F32) for i in range(2)]
        xs2 = [wview(16384 + i * 4096, [128, 2048], BF16) for i in range(2)]
        jk1 = [wview(24576 + i * 4096, [128, 2048], BF16) for i in range(2)]
        for tt in range(NT):
            r0 = s * S + tt * 128
            xb = tt % 2
            DMA("sp", xt[xb], x[r0:r0 + 128, :], [], [("xt", xb)], "d_xt%d" % xb)
            if stop_after == "p1x":
                continue
            xs, xsk = xs2[xb], ("xs", xb)
            rms_stats(xt[xb], [("xt", xb)], 2048, xb, jk1[xb], ("jk1", xb))
            if stop_after == "p1a":
                continue
            TS("dve", xs, xt[xb], sm[:, xb:xb + 1], None, ALU.mult, None, [("xt", xb), ("sm", xb)], [xsk])
            if stop_after == "p1b":
                continue
            transpose_to(xs, [xsk], 16,
                         lambda c, tt=tt: hT[:, c, tt * 128:(tt + 1) * 128],
                         lambda c: pp_t[:, c:c + 1],
                         lambda c: [hk(c)])
        def rg_loads(n_):
            sx_ = load_w(win_b, n_ * 256, 256, ["win"])
            sg_ = load_w(win_b, 1024 + n_ * 256, 256, ["win"])
            for mi_ in range(4):
                src_ = rgw_b[mi_ * 1024 + n_ * 256: mi_ * 1024 + (n_ + 1) * 256, :].rearrange("(c p) o -> p c o", p=128)
                DMA("sp", rgw_t[:, mi_, :, :], src_, ["rgwb"], ["rgw_t"], "d_rgw")
            return sx_, sg_
        rg_pre = rg_loads(0)
        SC.barrier(keep=WKEYS)

        if stop_after in ("p1", "p1x", "p1a", "p1b"):
            break
        SP4 = (S + 4) * 4
        Bf = [wview(i * SP4, [128, S + 4], F32) for i in range(6)]
        off = 6 * SP4
        XB = [wview(off + i * S * 2, [128, S], BF16) for i in range(2)]
        off += 2 * S * 2
        MO = [wview(off, [128, S], BF16)]
        off += S * 2
        assert off <= WBYTES, off
        mo_i = Rot(1)
        XR = [0, 1]
        XC = [2, 3]
        for n in range(4):
            sx, sg = rg_pre if n == 0 else rg_loads(n)
            for cc in range(2):
                ch = n * 2 + cc
                xr = Bf[XR[cc]]
                kxr = ("B", XR[cc])
                MS("pool", xr[:, 0:2], 0.0, [kxr])
                MS("pool", xr[:, S + 2:S + 4], 0.0, [kxr])

                def ev_x(blk, b, xr=xr, kxr=kxr):
                    dst = xr[:, 2 + blk * 512: 2 + (blk + 1) * 512]
                    if blk % 2 == 0:
                        AC(dst, ps[:, b, :], AF.Copy, [PS(b)], [kxr])
                    else:
                        CP("dve", dst, ps[:, b, :], [PS(b)], [kxr])
                proj_fm(sx, cc * 128, 128, ev_x)
                xc = Bf[XC[cc]]
                kxc = ("B", XC[cc])
                TS("dve", xc[:, 0:S], xr[:, 0:S], pp_t[:, 32 + ch:33 + ch], pp_t[:, 64 + ch:65 + ch], ALU.mult, ALU.add,
                   [kxr, "pp"], [kxc])
                for j in range(1, 4):
                    STT(xc[:, 0:S], xr[:, j:j + S], pp_t[:, 32 + j * 8 + ch:33 + j * 8 + ch], xc[:, 0:S], ALU.mult, ALU.add,
                        [kxr, kxc, "pp"], [kxc])
                CP("pool", XB[cc][:, :], xc[:, 0:S], [kxc], [("XB", cc)])
            for oc in range(2):
                ch = n * 2 + oc
                xc = Bf[XC[oc]]
                kxc = ("B", XC[oc])
                bank_u = Rot(4, 0)
                for d in range(2):
                    for gi in range(2):
                        mi = 2 * d + gi
                        bcol = 72 + 16 * d + 8 * gi + ch
                        gdst = Bf[1] if gi == 0 else Bf[4]
                        gkey = ("B", 1) if gi == 0 else ("B", 4)
                        for blk in range(NB):
                            b = bank_u.next()
                            for cc in range(2):
                                MM(ps[:, b, :], rgw_t[:, mi, cc, oc * 128:(oc + 1) * 128], XB[cc][:, blk * 512:(blk + 1) * 512],
                                   cc == 0, cc == 1, ["rgw_t", ("XB", cc)], [PS(b)], signal=(cc == 1))
                            AC(gdst[:, blk * 512:(blk + 1) * 512], ps[:, b, :], AF.Sigmoid, [PS(b), "pp"], [gkey],
                               bias=pp_t[:, bcol:bcol + 1])
                    if d == 0:
                        for blk in range(NB):
                            bgp = 4 + blk
                            for k in range(16):
                                MM(ps[:, bgp, :], wbuf[sg][:, k, oc * 128:(oc + 1) * 128], hT[:, k, blk * 512:(blk + 1) * 512],
                                   k == 0, k == 15, [("wb", sg), hk(k)], [PS(bgp)], signal=(k == 15))
                    a_, u_, tmp = Bf[1][:, 0:S], Bf[4][:, 0:S], Bf[0][:, 0:S]
                    ka, ku, kt = ("B", 1), ("B", 4), ("B", 0)
                    ccol = 8 * d + ch
                    TT("pool", u_, u_, xc[:, 0:S], ALU.mult, [ku, kxc], [ku])
                    AC(tmp, a_, AF.Exp, [ka, "ppx2"], [kt], scale=ppx2[:, ccol:ccol + 1])
                    AC(a_, a_, AF.Exp, [ka, "ppx"], [ka], scale=ppx[:, ccol:ccol + 1])
                    AC(tmp, tmp, AF.Sqrt, [kt, "one"], [kt], bias=one_t[:, 0:1], scale=-1.0)
                    TT("dve", u_, u_, tmp, ALU.mult, [ku, kt], [ku])
                    if d == 0:
                        SCAN(Bf[5][:, 0:S], a_, u_, [ka, ku], [("B", 5)])
                    else:
                        SCAN(Bf[0][:, 0:S][:, ::-1], a_[:, ::-1], u_[:, ::-1], [ka, ku], [("B", 0)])
                TT("dve", Bf[5][:, 0:S], Bf[5][:, 0:S], Bf[0][:, 0:S], ALU.add, [("B", 5), ("B", 0)], [("B", 5)])
                for blk in range(NB):
                    AC(Bf[1][:, blk * 512:(blk + 1) * 512], ps[:, 4 + blk, :], AF.Gelu, [PS(4 + blk)], [("B", 1)])
                mi_ = mo_i.next()
                TT("dve", MO[mi_][:, :], Bf[5][:, 0:S], Bf[1][:, 0:S], ALU.mult, [("B", 5), ("B", 1)], [("MO", mi_)])
                row0 = (s * 16 + ch) * 128
                DMA("sp", mix_scr[row0:row0 + 128, :], MO[mi_][:, :], [("MO", mi_)], [("mix", s, ch)], "d_mo%d" % mi_)
        def gla_loads(h_):
            sqk_ = load_w(win_b, 2048 + h_ * 128, 128, ["win"])
            load_w(win_b, 2560 + h_ * 128, 128, ["win"], slot=sqk_, sub0=128)
            skv_ = load_w(win_b, 2560 + h_ * 128, 128, ["win"])
            load_w(win_b, 3072 + h_ * 256, 256, ["win"], slot=skv_, sub0=128)
            sgg_ = load_w(win_b, 4096 + h_ * 256, 256, ["win"])
            return sqk_, skv_, sgg_
        DMA("sp", wlr_t[:], win_b[:, 5120:5152].rearrange("(k p) c -> p k c", p=128), ["win"], ["wlr"], "d_wlr")
        gla_pre = gla_loads(0)
        SC.barrier(keep=WKEYS)

        if stop_after == "rg":
            break
        o = [0]

        def carve(shape, dtype):
            n_ = 1
            for s_ in shape[1:]:
                n_ *= s_
            v = wview(o[0], shape, dtype)
            o[0] += ((n_ * (4 if dtype == F32 else 2) + 3) // 4) * 4
            return v
        lrT_all = carve([128, S], BF16)
        qeT = [carve([128, S], BF16) for _ in range(2)]
        keT = [carve([128, S], BF16) for _ in range(2)]
        ks = [carve([128, NT, 128], BF16) for _ in range(2)]
        v_sb = carve([128, NT, 256], BF16)
        o_acc = carve([128, NT, 256], F32)
        dec = [carve([128, NT], F32) for _ in range(2)]
        lpair = carve([128, 256], F32)
        eqpair = carve([128, 256], F32)
        l_t = [lpair[:, d * 128:(d + 1) * 128] for d in range(2)]
        eq_t = [eqpair[:, d * 128:(d + 1) * 128] for d in range(2)]
        ek_t = [carve([128, 128], F32) for _ in range(2)]
        gs2 = [lpair, eqpair]
        gs2k = [[("l", 0), ("l", 1)], [("eq", 0), ("eq", 1)]]
        gob2 = [ek_t[i].bitcast(BF16) for i in range(2)]
        ksf_t = [carve([128, 128], F32) for _ in range(2)]
        at_t = [carve([128, 128], BF16) for _ in range(2)]
        S32 = [carve([128, 256], F32) for _ in range(2)]
        Sbf = [[carve([128, 256], BF16) for _ in range(2)] for _ in range(2)]
        assert o[0] <= WBYTES, o[0]

        for d in range(2):
            MS("pool", lrT_all[32 * d:32 * d + 32, :], 1.0, [("lrT", d)])
            for blk in range(NB):
                b = bank.next()
                for k in range(16):
                    MM(ps[0:16, b, :], wlr_t[:, k, d * 16:(d + 1) * 16], hT[:, k, blk * 512:(blk + 1) * 512],
                       k == 0, k == 15, ["wlr", hk(k)], [PS(b)], signal=(k == 15))
                AC(lrT_all[32 * d:32 * d + 16, blk * 512:(blk + 1) * 512], ps[0:16, b, :], AF.Copy, [PS(b)], [("lrT", d)])

        for h in range(4):
            sqk, skv, sgg = gla_pre if h == 0 else gla_loads(h)
            def kv_tok(c_):
                bt_ = 4 + c_ % 2
                tsl_ = slice(c_ * 128, (c_ + 1) * 128)
                for k in range(16):
                    MM(ps[:, bt_, 0:384], hT[:, k, tsl_], wbuf[skv][:, k, 0:384], k == 0, k == 15,
                       [("wb", skv), hk(k)], [PS(bt_)], signal=(k == 15))
            for blk in range(NB):
                bq, bk = (0, 1) if blk % 2 == 0 else (2, 3)
                for (bb, sub) in ((bq, 0), (bk, 128)):
                    for k in range(16):
                        MM(ps[:, bb, :], wbuf[sqk][:, k, sub:sub + 128], hT[:, k, blk * 512:(blk + 1) * 512],
                           k == 0, k == 15, [("wb", sqk), hk(k)], [PS(bb)], signal=(k == 15))
                if blk == 0:
                    kv_tok(0)
                for ti in range(4):
                    c = blk * 4 + ti
                    tsl = slice(c * 128, (c + 1) * 128)
                    psl = slice(ti * 128, (ti + 1) * 128)
                    bt = 4 + c % 2
                    bxc = [6, 7]
                    for d in range(2):
                        MM(ps[:, bxc[d], 256:384], lrT_all[32 * d:32 * d + 17, tsl], gkw_t[32 * d:32 * d + 17, h * 128:(h + 1) * 128],
                           True, True, [("lrT", d), "gkw"], [PS(bxc[d])])
                    if c + 1 < NT:
                        kv_tok(c + 1)
                    for d in range(2):
                        AC(l_t[d][:, :], ps[:, bxc[d], 256:384], AF.Exp, [PS(bxc[d])], [("l", d)], scale=-1.0)
                    for d in range(2):
                        AC(l_t[d][:, :], l_t[d][:, :], AF.Ln, [("l", d), "one"], [("l", d)], bias=one_t[:, 0:1])
                    for d in range(2):
                        bc = bxc[d]
                        MM(ps[:, bc, 0:128], l_t[d][:, :], Ucum[d], True, True, [("l", d), "cst"], [PS(bc)], signal=False)
                        MM(ps[:, bc, 128:256], Mks[d], l_t[d][:, :], True, True, [("l", d), "cst"], [PS(bc)])
                    AC(v_sb[:, c, :], ps[:, bt, 128:384], AF.Copy, [PS(bt)], [("v", c)])
                    for d in range(2):
                        bc = bxc[d]
                        AC(eq_t[d][:, :], ps[:, bc, 0:128], AF.Exp, [PS(bc)], [("eq", d)])
                        AC(ek_t[d][:, :], ps[:, bc, 0:128], AF.Exp, [PS(bc)], [("ek", d)], scale=-1.0)
                        lastc = 127 if d == 0 else 0
                        AC(dec[d][:, c:c + 1], ps[:, bc, lastc:lastc + 1], AF.Exp, [PS(bc)], [("dec", d)])
                        AC(ksf_t[d][:, :], ps[:, bc, 128:256], AF.Exp, [PS(bc)], [("ksf", d)])
                    for d in range(2):
                        STT(qeT[d][:, tsl], ps[:, bq, psl], 128.0 ** -0.5, eq_t[d][:, :], ALU.mult, ALU.mult,
                            [PS(bq), ("eq", d)], [("qe", d, c)])
                        TT("dve", keT[d][:, tsl], ps[:, bk, psl], ek_t[d][:, :], ALU.mult, [PS(bk), ("ek", d)], [("ke", d, c)])
                        TT("dve", ks[d][:, c, :], ps[:, bt, 0:128], ksf_t[d][:, :], ALU.mult, [PS(bt), ("ksf", d)], [("ks", d, c)])
            touched = set()
            for i in range(NT):
                for d in range(2):
                    c = i if d == 0 else NT - 1 - i
                    tsl = slice(c * 128, (c + 1) * 128)
                    first = (i == 0)
                    if i < NT - 1:
                        bkv = bank.next()
                        MM(ps[:, bkv, 0:256], ks[d][:, c, :], v_sb[:, c, :], True, True, [("ks", d, c), ("v", c)], [PS(bkv)])
                        if first:
                            CP("dve", S32[d][:, :], ps[:, bkv, 0:256], [PS(bkv)], [("S32", d)])
                        else:
                            STT(S32[d][:, :], S32[d][:, :], dec[d][:, c:c + 1], ps[:, bkv, 0:256], ALU.mult, ALU.add,
                                [PS(bkv), ("S32", d), ("dec", d)], [("S32", d)])
                        AC(Sbf[d][i % 2][:, :], S32[d][:, :], AF.Copy, [("S32", d)], [("Sbf", d, i % 2)])
                    ba = bank.next()
                    MM(ps[:, ba, 0:128], keT[d][:, tsl], qeT[d][:, tsl], True, True, [("ke", d, c), ("qe", d, c)], [PS(ba)])
                    TT("dve", at_t[d][:, :], ps[:, ba, 0:128], maskT[d], ALU.mult, [PS(ba), "cst"], [("at", d)])
                    bo = bank.next()
                    MM(ps[:, bo, 0:256], at_t[d][:, :], v_sb[:, c, :], True, first, [("at", d), ("v", c)], [PS(bo)], signal=first)
                    if not first:
                        MM(ps[:, bo, 0:256], qeT[d][:, tsl], Sbf[d][(i - 1) % 2][:, :], False, True,
                           [("qe", d, c), ("Sbf", d, (i - 1) % 2)], [PS(bo)])
                    if c not in touched:
                        touched.add(c)
                        AC(o_acc[:, c, :], ps[:, bo, 0:256], AF.Copy, [PS(bo)], [("oa", c)])
                    else:
                        TT("dve", o_acc[:, c, :], ps[:, bo, 0:256], o_acc[:, c, :], ALU.add, [PS(bo), ("oa", c)], [("oa", c)])
            def g_mm(c_):
                tsl_ = slice(c_ * 128, (c_ + 1) * 128)
                bg_ = bank.next()
                for k in range(16):
                    MM(ps[:, bg_, 0:256], hT[:, k, tsl_], wbuf[sgg][:, k, 0:256], k == 0, k == 15,
                       [("wb", sgg), hk(k)], [PS(bg_)], signal=(k == 15))
                return bg_
            bg_next = g_mm(0)
            for c in range(NT):
                tsl = slice(c * 128, (c + 1) * 128)
                bg = bg_next
                pi = c % 2
                gs_t, gsk, gob, gobk = gs2[pi], gs2k[pi], gob2[pi], ("ek", pi)
                smc = 8 + pi
                AC(gs_t[:, :], ps[:, bg, 0:256], AF.Silu, [PS(bg)], gsk)
                rms_stats_pool(o_acc[:, c, :], [("oa", c)], 256, smc, gob[:, :], gobk)
                STT(o_acc[:, c, :], o_acc[:, c, :], sm[:, smc:smc + 1], gnorm_b[:, :], ALU.mult, ALU.mult,
                    [("oa", c), ("sm", smc), "gnorm_b"], [("oa", c)])
                TT("pool", gob[:, :], o_acc[:, c, :], gs_t[:, :], ALU.mult, [("oa", c)] + gsk, [gobk])
                if c + 1 < NT:
                    bg_next = g_mm(c + 1)
                transpose_to(gob, [gobk], 2, lambda cc, tsl=tsl: qeT[cc][:, tsl], lambda cc: None,
                             lambda cc, c=c: [("qe", cc, c)])
            for cc in range(2):
                ch = 8 + h * 2 + cc
                row0 = (s * 16 + ch) * 128
                DMA("sp", mix_scr[row0:row0 + 128, :], qeT[cc][:, :], [("qe", cc, c) for c in range(NT)], [("mix", s, ch)], "d_mog%d" % cc)
        pre_wout = [load_w(wout_b, nb * 512, 512, ["wout"]) for nb in range(2)] if s > 0 else None
        SC.barrier(keep=WKEYS)

        pump(1000)
        if stop_after == "gla":
            break
        mixT = regA[:, 0:8192].rearrange("p (k t) -> p k t", k=16)
        hfT = regA[:, 8192:16384].rearrange("p (k t) -> p k t", k=16)
        aT = regA[:, 16384:24576].rearrange("p (k t) -> p k t", k=16)
        hf2 = [regA[:, 24576 + i * 2048:24576 + (i + 1) * 2048] for i in range(3)]
        rl_t = [regA[:, 30720 + i * 1024: 30720 + (i + 1) * 1024].bitcast(F32) for i in range(2)]

        def junk_of(ti):
            return regA[:, 16384 + ti * 2048:16384 + (ti + 1) * 2048], [("aT", 4 * ti + i) for i in range(4)]

        def load_mixT(blk_):
            src_ = mix_scr[s * 2048:(s + 1) * 2048, blk_ * 512:(blk_ + 1) * 512].rearrange("(k p) t -> p k t", p=128)
            DMA("sp", mixT, src_, [("mix", s, ch) for ch in range(16)], ["mixT"], "d_mixT")
        load_mixT(0)
        x1 = wview(0, [128, 4, 2048], F32)
        ft = wview(32768, [128, 4, 2048], F32)
        rl_i = Rot(2)
        if pre_wout is None:
            pre_wout = [load_w(wout_b, nb * 512, 512, ["wout"]) for nb in range(2)]
        for blk in range(NB):
            t0 = s * S + blk * 512
            for ti in range(4):
                DMA("pool", x1[:, ti, :], x[t0 + ti * 128: t0 + (ti + 1) * 128, :], [], [("x1", ti)], "d_x1%d" % ti)
            def s1(ti):
                jk, jkk = junk_of(ti)
                c1 = 16 + ti
                rms_stats(ft[:, ti, :], [("ft", ti)], 2048, c1, jk, jkk[0])
                STT(ft[:, ti, :], ft[:, ti, :], sm[:, c1:c1 + 1], wpost_b[:, :], ALU.mult, ALU.mult,
                    [("ft", ti), ("sm", c1), "wpost_b"] + jkk[1:], [("ft", ti)])
                TT("pool" if ti % 2 == 0 else "dve", x1[:, ti, :], x1[:, ti, :], ft[:, ti, :], ALU.add,
                   [("x1", ti), ("ft", ti)], [("x1", ti)])

            def s2a(ti):
                jk, jkk = junk_of(ti)
                c2 = 20 + ti
                rms_stats(x1[:, ti, :], [("x1", ti)], 2048, c2, jk, jkk[0])
                TS("dve", hf2[ti % 3], x1[:, ti, :], sm[:, c2:c2 + 1], None, ALU.mult, None,
                   [("x1", ti), ("sm", c2)], [("hf_t", ti % 3)])

            def s2b(ti):
                transpose_to(hf2[ti % 3], [("hf_t", ti % 3)], 16,
                             lambda c, ti=ti: hfT[:, c, ti * 128:(ti + 1) * 128],
                             lambda c: pp_t[:, 16 + c:17 + c],
                             lambda c: [("hfT", c)])
            for nb in range(4):
                sl = pre_wout[nb] if nb < 2 else load_w(wout_b, nb * 512, 512, ["wout"])
                for ti in range(4):
                    b = bank.next()
                    for k in range(16):
                        MM(ps[:, b, :], mixT[:, k, ti * 128:(ti + 1) * 128], wbuf[sl][:, k, :], k == 0, k == 15,
                           [("wb", sl), "mixT"], [PS(b)], signal=(k == 15))
                    dst = ft[:, ti, nb * 512:(nb + 1) * 512]
                    if (nb + ti) % 2 == 0:
                        AC(dst, ps[:, b, :], AF.Copy, [PS(b)], [("ft", ti)])
                    else:
                        CP("dve", dst, ps[:, b, :], [PS(b)], [("ft", ti)])
                    if nb == 3:
                        s1(ti)
                        if ti in (1, 2, 3):
                            s2a(ti - 1)
            s2b(0)
            s2b(1)
            s2b(2)
            s2a(3)
            s2b(3)
            for q in range(4):
                for jg in range(4):
                    sl = load_w(wup_b, (q * 16 + jg * 4) * 128, 512, ["wup"])
                    if q == 0 and jg == 2 and blk + 1 < NB:
                        load_mixT(blk + 1)
                    for jj in range(4):
                        j = jg * 4 + jj
                        b = bank.next()
                        for k in range(16):
                            MM(ps[:, b, :], wbuf[sl][:, k, jj * 128:(jj + 1) * 128], hfT[:, k, :], k == 0, k == 15,
                               [("wb", sl), ("hfT", k)], [PS(b)], signal=(k == 15))
                        ri = rl_i.next()
                        AC(rl_t[ri], ps[:, b, :], AF.Relu, [PS(b)], [("rl", ri)])
                        TT("pool" if j % 2 == 0 else "dve", aT[:, j, :], rl_t[ri], rl_t[ri], ALU.mult, [("rl", ri)], [("aT", j)])
                for nb in range(4):
                    sl = load_w(wdown_b, nb * 512, 512, ["wdown"], rows0=q * 2048)
                    for ti in range(4):
                        b = bank.next()
                        for j in range(16):
                            MM(ps[:, b, :], aT[:, j, ti * 128:(ti + 1) * 128], wbuf[sl][:, j, :], j == 0, j == 15,
                               [("wb", sl), ("aT", j)], [PS(b)], signal=(j == 15))
                        fsl = ft[:, ti, nb * 512:(nb + 1) * 512]
                        if q == 0:
                            AC(fsl, ps[:, b, :], AF.Copy, [PS(b)], [("ft", ti)])
                        else:
                            TT("dve", fsl, ps[:, b, :], fsl, ALU.add, [PS(b), ("ft", ti)], [("ft", ti)])
            if blk + 1 < NB:
                pre_wout = [load_w(wout_b, nb * 512, 512, ["wout"]) for nb in range(2)]
            for ti in range(4):
                jk, jkk = junk_of(ti)
                c3 = 24 + ti
                rms_stats(ft[:, ti, :], [("ft", ti)], 2048, c3, jk, jkk[0])
                STT(ft[:, ti, :], ft[:, ti, :], sm[:, c3:c3 + 1], wfpost_b[:, :], ALU.mult, ALU.mult,
                    [("ft", ti), ("sm", c3), "wfpost_b"] + jkk[1:], [("ft", ti)])
                TT("dve", ft[:, ti, :], ft[:, ti, :], x1[:, ti, :], ALU.add, [("x1", ti), ("ft", ti)], [("ft", ti)])
                DMA("pool", y[t0 + ti * 128: t0 + (ti + 1) * 128, :], ft[:, ti, :], [("ft", ti)], [("y", t0, ti)], "d_y%d" % ti)
        SC.barrier(keep=WKEYS)

    with nc.Block() as block:
        @block.tensor
        def _(e):
            SC.replay("pe", e)

        @block.scalar
        def _(e):
            SC.replay("act", e)

        @block.vector
        def _(e):
            SC.replay("dve", e)

        @block.gpsimd
        def _(e):
            SC.replay("pool", e)

        @block.sync
        def _(e):
            SC.replay("sp", e)
    return nc


def host_prep(inp):
    f = np.float32

    def colpack(v):
        v = np.asarray(v, f).reshape(-1, 128)
        return np.ascontiguousarray(v.T)
    cols = [colpack(inp["norm_mix_pre"][0]), colpack(inp["norm_ffn_pre"][0])]
    cw = np.asarray(inp["rg_conv_w"][0], f)
    for j in range(4):
        cols.append(colpack(cw[j]))
    cols.append(colpack(inp["rg_conv_b"][0]))
    for nm in ["rg_a_b_fwd", "rg_i_b_fwd", "rg_a_b_bwd", "rg_i_b_bwd", "rg_lambda_fwd", "rg_lambda_bwd"]:
        cols.append(colpack(inp[nm][0]))
    pp = np.ascontiguousarray(np.concatenate(cols, axis=1))
    assert pp.shape == (128, 120)
    bvec = np.concatenate([np.asarray(inp["norm_mix_post"][0], f), np.asarray(inp["norm_ffn_post"][0], f),
                           np.asarray(inp["gla_norm"][0], f)])[None, :]
    gkw = np.zeros((17, 1024), f)
    gkw[:16, :512] = inp["gla_gk_w_fwd"][0]
    gkw[16, :512] = inp["gla_gk_b_fwd"][0]
    gkw[:16, 512:] = inp["gla_gk_w_bwd"][0]
    gkw[16, 512:] = inp["gla_gk_b_bwd"][0]
    rgw = np.concatenate([np.asarray(inp[nm][0], f).reshape(1024, 256)
                          for nm in ["rg_a_w_fwd", "rg_i_w_fwd", "rg_a_w_bwd", "rg_i_w_bwd"]], axis=0)
    i_ = np.arange(128)
    tp, t = i_[:, None], i_[None, :]
    g = -1.0 / 16.0
    cst = np.concatenate([
        np.eye(128, dtype=f),
        (tp <= t).astype(f),
        (tp >= t).astype(f),
        (tp <= t).astype(f) * g,
        (tp >= t).astype(f) * g,
        (tp > t).astype(f) * g,
        (tp < t).astype(f) * g,
    ], axis=1).astype(f)
    return dict(
        w_in=np.ascontiguousarray(inp["w_in"][0], f), w_out=np.ascontiguousarray(inp["w_out"][0], f),
        w_up=np.ascontiguousarray(inp["w_up"][0], f), w_down=np.ascontiguousarray(inp["w_down"][0], f),
        rgw=np.ascontiguousarray(rgw), pp=pp, bvec=np.ascontiguousarray(bvec), gkw=gkw, cst=np.ascontiguousarray(cst))


_NC_CACHE = {}


def kernel(**inputs):
    S = 2048
    NSEQ = 3
    xp = np.asarray(inputs["x_prompt"], np.float32)
    xs = np.asarray(inputs["x_sample"], np.float32)
    allx = np.concatenate([xp, xs], axis=0)
    shared = host_prep(inputs)
    key = (S, NSEQ)
    if key not in _NC_CACHE:
        _NC_CACHE[key] = build(S, NSEQ)
    nc = _NC_CACHE[key]
    in_maps = []
    for c in range(NCORE):
        m = dict(shared)
        m["x"] = np.ascontiguousarray(allx[c * NSEQ:(c + 1) * NSEQ].reshape(NSEQ * S, D))
        in_maps.append(m)
    res = run_bass_kernel_spmd(nc, in_maps, core_ids=list(range(NCORE)))
    ys = np.concatenate([np.asarray(r["y"], np.float32).reshape(NSEQ, S, D) for r in res.results], axis=0)
    return ys[:8].copy(), ys[8:].copy()
```

```python
import numpy as np
import concourse.bass as bass
import concourse.mybir as mybir
from concourse.bass_utils import run_bass_kernel_spmd

F32 = mybir.dt.float32
BF16 = mybir.dt.bfloat16
AF = mybir.ActivationFunctionType
ALU = mybir.AluOpType

D = 2048
P_IN = 5152
DFF = 8192
NCORE = 8
EPS = 1e-6
ENG = ["pe", "act", "dve", "pool", "sp"]
import os
_EV = os.environ.get("EVSEL", "alt")


def EVAC_SEL(c):
    return {"alt": c % 2 == 0, "act": True, "dve": False}[_EV]


class Sched:
    def __init__(self, nc):
        self.nc = nc
        self.prog = {e: [] for e in ENG}
        self.semh = {}
        self.cnt = {}
        self.lastw = {}
        self.readers = {}
        self.known = {e: {} for e in ENG}
        for e in ["pe", "act", "dve", "pool"]:
            self._mk(e)

    def _mk(self, name):
        if name not in self.semh:
            self.semh[name] = self.nc.alloc_semaphore(name="s_" + name)
            self.cnt[name] = 0

    def op(self, eng, fn, reads=(), writes=(), signal=True, dsem=None):
        pr = [k for k in reads if isinstance(k, tuple) and k[0] == "ps"]
        if pr:
            reads = [k for k in reads if k not in pr]
            writes = list(writes) + pr
        deps = {}

        def add(tok):
            if tok is not None:
                deps[tok[0]] = max(deps.get(tok[0], 0), tok[1])

        for k in reads:
            add(self.lastw.get(k))
        for k in writes:
            add(self.lastw.get(k))
            for sn, v in self.readers.get(k, {}).items():
                add((sn, v))
        waits = []
        for sn, v in deps.items():
            if sn == "pe" and eng == "pe":
                continue
            if self.known[eng].get(sn, 0) >= v:
                continue
            self.known[eng][sn] = v
            waits.append((sn, v))
        if dsem is not None:
            self._mk(dsem)
            self.cnt[dsem] += 16
            tok = (dsem, self.cnt[dsem])
            rec = (waits, fn, dsem, 16)
        elif signal:
            self.cnt[eng] += 1
            tok = (eng, self.cnt[eng])
            rec = (waits, fn, eng, 1)
        else:
            tok = (eng, self.cnt[eng] + 1)
            rec = (waits, fn, None, 0)
        self.prog[eng].append(rec)
        for k in reads:
            r = self.readers.setdefault(k, {})
            r[tok[0]] = max(r.get(tok[0], 0), tok[1])
        for k in writes:
            self.lastw[k] = tok
            self.readers[k] = {}

    def barrier(self, keep=()):
        kept = {k: self.lastw[k] for k in keep if k in self.lastw}
        skip = {tok[0] for tok in kept.values()}
        for e in ENG:
            waits = []
            for sn, c in self.cnt.items():
                if c == 0 or (sn == "pe" and e == "pe") or sn in skip:
                    continue
                if self.known[e].get(sn, 0) >= c:
                    continue
                self.known[e][sn] = c
                waits.append((sn, c))
            if waits:
                self.prog[e].append((waits, None, None, 0))
        self.lastw = dict(kept)
        self.readers = {}

    def replay(self, eng, e):
        for waits, fn, sn, inc in self.prog[eng]:
            for wsn, v in waits:
                e.wait_ge(self.semh[wsn], v)
            if fn is None:
                continue
            ins = fn(e)
            if sn is not None:
                ins.then_inc(self.semh[sn], inc)


def build(S, NSEQ, stop_after=None):
    nc = bass.Bass("TRN2", target_bir_lowering=False)
    NT = S // 128
    NB = S // 512
    T = NSEQ * S
    dt = nc.dram_tensor
    x = dt("x", [T, D], F32, kind="ExternalInput").ap()
    w_in = dt("w_in", [D, P_IN], F32, kind="ExternalInput").ap()
    w_out = dt("w_out", [D, D], F32, kind="ExternalInput").ap()
    w_up = dt("w_up", [D, DFF], F32, kind="ExternalInput").ap()
    w_down = dt("w_down", [DFF, D], F32, kind="ExternalInput").ap()
    rgw = dt("rgw", [4096, 256], F32, kind="ExternalInput").ap()
    pp = dt("pp", [128, 120], F32, kind="ExternalInput").ap()
    bvec = dt("bvec", [1, 4352], F32, kind="ExternalInput").ap()
    gkw = dt("gkw", [17, 1024], F32, kind="ExternalInput").ap()
    cst = dt("cst", [128, 7 * 128], F32, kind="ExternalInput").ap()
    y = dt("y", [T, D], F32, kind="ExternalOutput").ap()
    win_b = dt("win_b", [D, P_IN], BF16, kind="Internal").ap()
    wout_b = dt("wout_b", [D, D], BF16, kind="Internal").ap()
    wup_b = dt("wup_b", [D, DFF], BF16, kind="Internal").ap()
    wdown_b = dt("wdown_b", [DFF, D], BF16, kind="Internal").ap()
    rgw_b = dt("rgw_b", [4096, 256], BF16, kind="Internal").ap()
    mix_scr = dt("mix_scr", [NSEQ * 16 * 128, S], BF16, kind="Internal").ap()

    sb = nc.alloc_sbuf_tensor
    pp_t = sb("pp_t", [128, 120], F32)
    ppx = sb("ppx", [128, 16], F32)
    ppx2 = sb("ppx2", [128, 16], F32)
    cst_t = sb("cst_t", [128, 7 * 128], F32)
    idb = sb("idb", [128, 128], BF16)
    bv_t = sb("bv_t", [1, 512], F32)
    ones1 = sb("ones1", [1, 128], F32)
    wpost_b = sb("wpost_b", [128, 2048], F32)
    wfpost_b = sb("wfpost_b", [128, 2048], F32)
    gnorm_b = sb("gnorm_b", [128, 256], F32)
    gkw_t = sb("gkw_t", [64, 512], BF16)
    eps_t = sb("eps_t", [128, 1], F32)
    one_t = sb("one_t", [128, 1], F32)
    mhalf_t = sb("mhalf_t", [128, 1], F32)
    sm = sb("sm", [128, 32], F32)
    wbuf = [sb("wbuf%d" % i, [128, 16, 512], BF16) for i in range(3)]
    rgw_t = sb("rgw_t", [128, 4, 2, 256], BF16)
    wlr_t = sb("wlr_t", [128, 16, 32], BF16)
    regA = sb("regA", [128, 16 * 2048], BF16)
    WBYTES = 64 * 1024
    regW = sb("regW", [128, WBYTES // 4], F32)
    ps = nc.alloc_psum_tensor("ps", [128, 8, 512], F32)

    SC = Sched(nc)
    op = SC.op

    def MM(out, lhsT, rhs, start, stop, reads, writes, signal=True):
        op("pe", lambda e: e.matmul(out, lhsT, rhs, start=start, stop=stop), reads, writes, signal)

    def TR(out, in_, reads, writes, signal=True):
        op("pe", lambda e: e.transpose(out, in_, idb[:]), reads, writes, signal)

    def AC(out, in_, func, reads, writes, bias=None, scale=1.0, accum=None):
        kw = {}
        if bias is not None:
            kw["bias"] = bias
        if accum is not None:
            kw["accum_out"] = accum
        op("act", lambda e: e.activation(out=out, in_=in_, func=func, scale=scale, **kw), reads, writes)

    def TS(eng, out, in0, s1, s2, op0, op1, reads, writes):
        if op1 is None:
            op(eng, lambda e: e.tensor_scalar(out=out, in0=in0, scalar1=s1, scalar2=None, op0=op0), reads, writes)
        else:
            op(eng, lambda e: e.tensor_scalar(out=out, in0=in0, scalar1=s1, scalar2=s2, op0=op0, op1=op1), reads, writes)

    def STT(out, in0, scalar, in1, op0, op1, reads, writes):
        op("dve", lambda e: e.scalar_tensor_tensor(out=out, in0=in0, scalar=scalar, in1=in1, op0=op0, op1=op1), reads, writes)

    deferred = []
    pump_ctr = [0]

    def pump(n=1):
        for _ in range(n):
            if deferred:
                d_, s_, key = deferred.pop(0)
                DMA("pool", d_, s_, [], [key, ("castslot", cast_i[0] % 3)], "c_" + key)
                cast_i[0] += 1

    def TT(eng, out, in0, in1, aop, reads, writes):
        op(eng, lambda e: e.tensor_tensor(out=out, in0=in0, in1=in1, op=aop), reads, writes)
        if eng == "pool":
            pump_ctr[0] += 1
            if pump_ctr[0] % 3 == 0:
                pump()

    def CP(eng, out, in_, reads, writes):
        op(eng, lambda e: e.tensor_copy(out=out, in_=in_), reads, writes)

    def MS(eng, ap, val, writes):
        op(eng, lambda e: e.memset(ap, val), [], writes)

    def SCAN(out, d0, d1, reads, writes):
        op("dve", lambda e: e.tensor_tensor_scan(out=out, data0=d0, data1=d1, initial=0.0, op0=ALU.mult, op1=ALU.add), reads, writes)

    def RECIP(out, in_, reads, writes):
        op("dve", lambda e: e.reciprocal(out=out, in_=in_), reads, writes)

    def DMA(eng, out, in_, reads, writes, dsem):
        op(eng, lambda e: e.dma_start(out=out, in_=in_), reads, writes, dsem=dsem)

    def PS(b):
        return ("ps", b)

    class Rot:
        def __init__(self, n, base=0):
            self.n, self.i, self.base = n, 0, base

        def next(self):
            v = self.base + self.i % self.n
            self.i += 1
            return v

    bank = Rot(8)
    wslot = Rot(3)

    def wview(off_bytes, shape, dtype):
        esz = 4 if dtype == F32 else 2
        n = 1
        for s_ in shape[1:]:
            n *= s_
        assert off_bytes % 4 == 0 and (n * esz) % 4 == 0
        assert off_bytes + n * esz <= WBYTES, (off_bytes, n * esz)
        v = regW[:, off_bytes // 4:(off_bytes + n * esz) // 4]
        if dtype != F32:
            v = v.bitcast(dtype)
        if len(shape) == 3:
            v = v.rearrange("p (a b) -> p a b", a=shape[1])
        return v


    cast_i = [0]

    def cast_w(src, dst, rows, cols, key, defer=False):
        piece = cols
        while piece > 2048:
            piece //= 2
        npc = cols // piece
        R = 256
        for r0 in range(0, rows, R):
            s_ = src[r0:r0 + R, :].rearrange("r (a b) -> r a b", a=npc)
            d_ = dst[r0:r0 + R, :].rearrange("r (a b) -> r a b", a=npc)
            if defer:
                deferred.append((d_, s_, key))
                continue
            DMA("pool", d_, s_, [], [key, ("castslot", cast_i[0] % 3)], "c_" + key)
            cast_i[0] += 1

    cast_w(w_in, win_b, D, P_IN, "win")
    cast_w(rgw, rgw_b, 4096, 256, "rgwb")
    cast_w(w_out, wout_b, D, D, "wout", defer=True)
    cast_w(w_up, wup_b, D, DFF, "wup", defer=True)
    cast_w(w_down, wdown_b, DFF, D, "wdown", defer=True)

    DMA("sp", pp_t[:], pp[:, :], [], ["pp"], "d_pp")
    DMA("sp", cst_t[:], cst[:, :], [], ["cst"], "d_cst")
    gkw_f = wview(0, [128, 512], F32)
    DMA("sp", gkw_f[0:17, :], gkw[:, 0:512], [], ["gkwf"], "d_gk")
    DMA("sp", gkw_f[32:49, :], gkw[:, 512:1024], [], ["gkwf"], "d_gk")
    MS("dve", eps_t[:], EPS, ["eps"])
    MS("dve", one_t[:], 1.0, ["one"])
    MS("dve", mhalf_t[:], -0.5, ["mhalf"])
    MS("dve", ones1[:], 1.0, ["ones1"])
    CP("dve", idb[:], cst_t[:, 0:128], ["cst"], ["idb"])
    CP("dve", gkw_t[0:17, :], gkw_f[0:17, :], ["gkwf"], ["gkw"])
    CP("dve", gkw_t[32:49, :], gkw_f[32:49, :], ["gkwf"], ["gkw"])
    AC(ppx[:], pp_t[:, 104:120], AF.Exp, ["pp"], ["ppx"], scale=-1.0)
    AC(ppx[:], ppx[:], AF.Ln, ["ppx", "one"], ["ppx"], bias=one_t[:, 0:1])
    TS("dve", ppx[:], ppx[:], -8.0, None, ALU.mult, None, ["ppx"], ["ppx"])
    TS("dve", ppx2[:], ppx[:], 2.0, None, ALU.mult, None, ["ppx"], ["ppx2"])
    for (dst, off, n) in [(wpost_b, 0, 2048), (wfpost_b, 2048, 2048), (gnorm_b, 4096, 256)]:
        DMA("sp", dst[:, :], bvec[0:1, off:off + n].broadcast_to([128, n]), [], [dst.name], "d_bv")
    maskT = [cst_t[:, 128:256], cst_t[:, 256:384]]
    Ucum = [cst_t[:, 384:512], cst_t[:, 512:640]]
    Mks = [cst_t[:, 640:768], cst_t[:, 768:896]]

    WKEYS = ["win", "rgwb", "wout", "wup", "wdown"]
    SC.barrier(keep=WKEYS)
    hT = regA[:].rearrange("p (k t) -> p k t", k=16)[:, :, 0:S]

    def hk(k):
        return ("hT", k)

    def rms_stats(src_ap, src_keys, n, col, junk, junk_key):
        cs = sm[:, col:col + 1]
        AC(junk, src_ap, AF.Square, src_keys, [junk_key, ("sm", col)], accum=cs)
        AC(cs, cs, AF.Sqrt, [("sm", col), "eps"], [("sm", col)], bias=eps_t[:, 0:1], scale=1.0 / n)
        RECIP(cs, cs, [("sm", col)], [("sm", col)])

    def rms_stats_pool(src_ap, src_keys, n, col, junk, junk_key):
        cs = sm[:, col:col + 1]
        AC(junk, src_ap, AF.Square, src_keys, [junk_key, ("sm", col)], accum=cs)
        TS("pool", cs, cs, 1.0 / n, EPS, ALU.mult, ALU.add, [("sm", col)], [("sm", col)])
        TT("pool", cs, cs, mhalf_t[:, 0:1], ALU.pow, [("sm", col), "mhalf"], [("sm", col)])

    def transpose_to(src_bf, src_keys, nchunk, dst_fn, scale_fn, dst_keys_fn):
        for g0 in range(0, nchunk, 4):
            b = bank.next()
            pb = ps[:, b, :].bitcast(BF16)
            gn = min(4, nchunk - g0)
            for i in range(gn):
                c = g0 + i
                TR(pb[:, i * 128:(i + 1) * 128], src_bf[:, c * 128:(c + 1) * 128], list(src_keys) + ["idb"], [PS(b)], signal=(i == gn - 1))
            for i in range(gn):
                c = g0 + i
                sc = scale_fn(c)
                src = pb[:, i * 128:(i + 1) * 128]
                if (g0 // 4) % 2 == 0:
                    if sc is None:
                        AC(dst_fn(c), src, AF.Copy, [PS(b)], dst_keys_fn(c))
                    else:
                        AC(dst_fn(c), src, AF.Copy, [PS(b), "pp"], dst_keys_fn(c), scale=sc)
                else:
                    if sc is None:
                        CP("dve", dst_fn(c), src, [PS(b)], dst_keys_fn(c))
                    else:
                        TS("dve", dst_fn(c), src, sc, None, ALU.mult, None, [PS(b), "pp"], dst_keys_fn(c))

    def load_w(src2d, col0, ncols, key_reads, slot=None, sub0=0, rows0=0):
        if slot is None:
            slot = wslot.next()
        src = src2d[rows0:rows0 + 2048, col0:col0 + ncols].rearrange("(k p) c -> p k c", p=128)
        DMA("sp", wbuf[slot][:, :, sub0:sub0 + ncols], src, key_reads, [("wb", slot)], "d_wb%d" % slot)
        return slot

    def proj_fm(slot, sub0, m, evac):
        for blk in range(NB):
            b = bank.next()
            for k in range(16):
                MM(ps[0:m, b, :], wbuf[slot][:, k, sub0:sub0 + m], hT[:, k, blk * 512:(blk + 1) * 512],
                   k == 0, k == 15, [("wb", slot), hk(k)], [PS(b)], signal=(k == 15))
            evac(blk, b)

    for s in range(NSEQ if stop_after != "setup" else 0):
        xt = [wview(i * 8192, [128, 2048], F32) for i in range(4)]
        xs2 = [wview(32768 + i * 4096, [128, 2048], BF16) for i in range(2)]
        jk1 = [wview(40960 + i * 4096, [128, 2048], BF16) for i in range(2)]
        for tt in range(NT):
            r0 = s * S + tt * 128
            xb = tt % 4
            DMA("sp", xt[xb], x[r0:r0 + 128, :], [], [("xt", xb)], "d_xt%d" % xb)
            if stop_after == "p1x":
                continue
            xs, xsk = xs2[xb % 2], ("xs", xb % 2)
            rms_stats(xt[xb], [("xt", xb)], 2048, xb, jk1[xb % 2], ("jk1", xb % 2))
            if stop_after == "p1a":
                continue
            TS("dve", xs, xt[xb], sm[:, xb:xb + 1], None, ALU.mult, None, [("xt", xb), ("sm", xb)], [xsk])
            if stop_after == "p1b":
                continue
            transpose_to(xs, [xsk], 16,
                         lambda c, tt=tt: hT[:, c, tt * 128:(tt + 1) * 128],
                         lambda c: pp_t[:, c:c + 1],
                         lambda c: [hk(c)])
        def rg_loads(n_):
            sx_ = load_w(win_b, n_ * 256, 256, ["win"])
            sg_ = load_w(win_b, 1024 + n_ * 256, 256, ["win"])
            for mi_ in range(4):
                src_ = rgw_b[mi_ * 1024 + n_ * 256: mi_ * 1024 + (n_ + 1) * 256, :].rearrange("(c p) o -> p c o", p=128)
                DMA("sp", rgw_t[:, mi_, :, :], src_, ["rgwb"], ["rgw_t"], "d_rgw")
            return sx_, sg_
        rg_pre = rg_loads(0)
        SC.barrier(keep=WKEYS)

        if stop_after in ("p1", "p1x", "p1a", "p1b"):
            break
        SP4 = (S + 4) * 4
        Bf = [wview(i * SP4, [128, S + 4], F32) for i in range(6)]
        off = 6 * SP4
        XB = [wview(off + i * S * 2, [128, S], BF16) for i in range(2)]
        off += 2 * S * 2
        MO = [wview(off, [128, S], BF16)]
        off += S * 2
        assert off <= WBYTES, off
        mo_i = Rot(1)
        XR = [0, 1]
        XC = [2, 3]
        for n in range(4):
            sx, sg = rg_pre if n == 0 else rg_loads(n)
            for cc in range(2):
                ch = n * 2 + cc
                xr = Bf[XR[cc]]
                kxr = ("B", XR[cc])
                MS("pool", xr[:, 0:2], 0.0, [kxr])
                MS("pool", xr[:, S + 2:S + 4], 0.0, [kxr])

                def ev_x(blk, b, xr=xr, kxr=kxr):
                    dst = xr[:, 2 + blk * 512: 2 + (blk + 1) * 512]
                    if blk % 2 == 0:
                        AC(dst, ps[:, b, :], AF.Copy, [PS(b)], [kxr])
                    else:
                        CP("dve", dst, ps[:, b, :], [PS(b)], [kxr])
                proj_fm(sx, cc * 128, 128, ev_x)
                xc = Bf[XC[cc]]
                kxc = ("B", XC[cc])
                TS("dve", xc[:, 0:S], xr[:, 0:S], pp_t[:, 32 + ch:33 + ch], pp_t[:, 64 + ch:65 + ch], ALU.mult, ALU.add,
                   [kxr, "pp"], [kxc])
                for j in range(1, 4):
                    STT(xc[:, 0:S], xr[:, j:j + S], pp_t[:, 32 + j * 8 + ch:33 + j * 8 + ch], xc[:, 0:S], ALU.mult, ALU.add,
                        [kxr, kxc, "pp"], [kxc])
                CP("pool", XB[cc][:, :], xc[:, 0:S], [kxc], [("XB", cc)])
            for oc in range(2):
                ch = n * 2 + oc
                xc = Bf[XC[oc]]
                kxc = ("B", XC[oc])
                bank_u = Rot(4, 0)
                for d in range(2):
                    for gi in range(2):
                        mi = 2 * d + gi
                        bcol = 72 + 16 * d + 8 * gi + ch
                        gdst = Bf[1] if gi == 0 else Bf[4]
                        gkey = ("B", 1) if gi == 0 else ("B", 4)
                        for blk in range(NB):
                            b = bank_u.next()
                            for cc in range(2):
                                MM(ps[:, b, :], rgw_t[:, mi, cc, oc * 128:(oc + 1) * 128], XB[cc][:, blk * 512:(blk + 1) * 512],
                                   cc == 0, cc == 1, ["rgw_t", ("XB", cc)], [PS(b)], signal=(cc == 1))
                            AC(gdst[:, blk * 512:(blk + 1) * 512], ps[:, b, :], AF.Sigmoid, [PS(b), "pp"], [gkey],
                               bias=pp_t[:, bcol:bcol + 1])
                    if d == 0:
                        for blk in range(NB):
                            bgp = 4 + blk
                            for k in range(16):
                                MM(ps[:, bgp, :], wbuf[sg][:, k, oc * 128:(oc + 1) * 128], hT[:, k, blk * 512:(blk + 1) * 512],
                                   k == 0, k == 15, [("wb", sg), hk(k)], [PS(bgp)], signal=(k == 15))
                    a_, u_, tmp = Bf[1][:, 0:S], Bf[4][:, 0:S], Bf[0][:, 0:S]
                    ka, ku, kt = ("B", 1), ("B", 4), ("B", 0)
                    ccol = 8 * d + ch
                    TT("pool", u_, u_, xc[:, 0:S], ALU.mult, [ku, kxc], [ku])
                    AC(tmp, a_, AF.Exp, [ka, "ppx2"], [kt], scale=ppx2[:, ccol:ccol + 1])
                    AC(a_, a_, AF.Exp, [ka, "ppx"], [ka], scale=ppx[:, ccol:ccol + 1])
                    AC(tmp, tmp, AF.Sqrt, [kt, "one"], [kt], bias=one_t[:, 0:1], scale=-1.0)
                    TT("dve", u_, u_, tmp, ALU.mult, [ku, kt], [ku])
                    if d == 0:
                        SCAN(Bf[5][:, 0:S], a_, u_, [ka, ku], [("B", 5)])
                    else:
                        SCAN(Bf[0][:, 0:S][:, ::-1], a_[:, ::-1], u_[:, ::-1], [ka, ku], [("B", 0)])
                TT("dve", Bf[5][:, 0:S], Bf[5][:, 0:S], Bf[0][:, 0:S], ALU.add, [("B", 5), ("B", 0)], [("B", 5)])
                for blk in range(NB):
                    AC(Bf[1][:, blk * 512:(blk + 1) * 512], ps[:, 4 + blk, :], AF.Gelu, [PS(4 + blk)], [("B", 1)])
                mi_ = mo_i.next()
                TT("dve", MO[mi_][:, :], Bf[5][:, 0:S], Bf[1][:, 0:S], ALU.mult, [("B", 5), ("B", 1)], [("MO", mi_)])
                row0 = (s * 16 + ch) * 128
                DMA("sp", mix_scr[row0:row0 + 128, :], MO[mi_][:, :], [("MO", mi_)], [("mix", s, ch)], "d_mo%d" % mi_)
        def gla_loads(h_):
            sqk_ = load_w(win_b, 2048 + h_ * 128, 128, ["win"])
            load_w(win_b, 2560 + h_ * 128, 128, ["win"], slot=sqk_, sub0=128)
            skv_ = load_w(win_b, 2560 + h_ * 128, 128, ["win"])
            load_w(win_b, 3072 + h_ * 256, 256, ["win"], slot=skv_, sub0=128)
            sgg_ = load_w(win_b, 4096 + h_ * 256, 256, ["win"])
            return sqk_, skv_, sgg_
        DMA("sp", wlr_t[:], win_b[:, 5120:5152].rearrange("(k p) c -> p k c", p=128), ["win"], ["wlr"], "d_wlr")
        gla_pre = gla_loads(0)
        SC.barrier(keep=WKEYS)

        if stop_after == "rg":
            break
        o = [0]

        def carve(shape, dtype):
            n_ = 1
            for s_ in shape[1:]:
                n_ *= s_
            v = wview(o[0], shape, dtype)
            o[0] += ((n_ * (4 if dtype == F32 else 2) + 3) // 4) * 4
            return v
        lrT_all = carve([128, S], BF16)
        qeT = [carve([128, S], BF16) for _ in range(2)]
        keT = [carve([128, S], BF16) for _ in range(2)]
        ks = [carve([128, NT, 128], BF16) for _ in range(2)]
        v_sb = carve([128, NT, 256], BF16)
        o_acc = carve([128, NT, 256], F32)
        dec = [carve([128, NT], F32) for _ in range(2)]
        lpair = carve([128, 256], F32)
        eqpair = carve([128, 256], F32)
        l_t = [lpair[:, d * 128:(d + 1) * 128] for d in range(2)]
        eq_t = [eqpair[:, d * 128:(d + 1) * 128] for d in range(2)]
        ek_t = [carve([128, 128], F32) for _ in range(2)]
        gs2 = [lpair, eqpair]
        gs2k = [[("l", 0), ("l", 1)], [("eq", 0), ("eq", 1)]]
        gob2 = [ek_t[i].bitcast(BF16) for i in range(2)]
        ksf_t = [carve([128, 128], F32) for _ in range(2)]
        at_t = [carve([128, 128], BF16) for _ in range(2)]
        S32 = [carve([128, 256], F32) for _ in range(2)]
        Sbf = [[carve([128, 256], BF16) for _ in range(2)] for _ in range(2)]
        assert o[0] <= WBYTES, o[0]

        for d in range(2):
            MS("pool", lrT_all[32 * d:32 * d + 32, :], 1.0, [("lrT", d)])
            for blk in range(NB):
                b = bank.next()
                for k in range(16):
                    MM(ps[0:16, b, :], wlr_t[:, k, d * 16:(d + 1) * 16], hT[:, k, blk * 512:(blk + 1) * 512],
                       k == 0, k == 15, ["wlr", hk(k)], [PS(b)], signal=(k == 15))
                AC(lrT_all[32 * d:32 * d + 16, blk * 512:(blk + 1) * 512], ps[0:16, b, :], AF.Copy, [PS(b)], [("lrT", d)])

        for h in range(4):
            sqk, skv, sgg = gla_pre if h == 0 else gla_loads(h)
            def kv_tok(c_):
                bt_ = 4 + c_ % 2
                tsl_ = slice(c_ * 128, (c_ + 1) * 128)
                for k in range(16):
                    MM(ps[:, bt_, 0:384], hT[:, k, tsl_], wbuf[skv][:, k, 0:384], k == 0, k == 15,
                       [("wb", skv), hk(k)], [PS(bt_)], signal=(k == 15))
            for blk in range(NB):
                bq, bk = (0, 1) if blk % 2 == 0 else (2, 3)
                for (bb, sub) in ((bq, 0), (bk, 128)):
                    for k in range(16):
                        MM(ps[:, bb, :], wbuf[sqk][:, k, sub:sub + 128], hT[:, k, blk * 512:(blk + 1) * 512],
                           k == 0, k == 15, [("wb", sqk), hk(k)], [PS(bb)], signal=(k == 15))
                if blk == 0:
                    kv_tok(0)
                for ti in range(4):
                    c = blk * 4 + ti
                    tsl = slice(c * 128, (c + 1) * 128)
                    psl = slice(ti * 128, (ti + 1) * 128)
                    bt = 4 + c % 2
                    bxc = [6, 7]
                    for d in range(2):
                        MM(ps[:, bxc[d], 256:384], lrT_all[32 * d:32 * d + 17, tsl], gkw_t[32 * d:32 * d + 17, h * 128:(h + 1) * 128],
                           True, True, [("lrT", d), "gkw"], [PS(bxc[d])])
                    if c + 1 < NT:
                        kv_tok(c + 1)
                    for d in range(2):
                        AC(l_t[d][:, :], ps[:, bxc[d], 256:384], AF.Exp, [PS(bxc[d])], [("l", d)], scale=-1.0)
                    for d in range(2):
                        AC(l_t[d][:, :], l_t[d][:, :], AF.Ln, [("l", d), "one"], [("l", d)], bias=one_t[:, 0:1])
                    for d in range(2):
                        bc = bxc[d]
                        MM(ps[:, bc, 0:128], l_t[d][:, :], Ucum[d], True, True, [("l", d), "cst"], [PS(bc)], signal=False)
                        MM(ps[:, bc, 128:256], Mks[d], l_t[d][:, :], True, True, [("l", d), "cst"], [PS(bc)])
                    AC(v_sb[:, c, :], ps[:, bt, 128:384], AF.Copy, [PS(bt)], [("v", c)])
                    for d in range(2):
                        bc = bxc[d]
                        AC(eq_t[d][:, :], ps[:, bc, 0:128], AF.Exp, [PS(bc)], [("eq", d)])
                        AC(ek_t[d][:, :], ps[:, bc, 0:128], AF.Exp, [PS(bc)], [("ek", d)], scale=-1.0)
                        lastc = 127 if d == 0 else 0
                        AC(dec[d][:, c:c + 1], ps[:, bc, lastc:lastc + 1], AF.Exp, [PS(bc)], [("dec", d)])
                        AC(ksf_t[d][:, :], ps[:, bc, 128:256], AF.Exp, [PS(bc)], [("ksf", d)])
                    for d in range(2):
                        STT(qeT[d][:, tsl], ps[:, bq, psl], 128.0 ** -0.5, eq_t[d][:, :], ALU.mult, ALU.mult,
                            [PS(bq), ("eq", d)], [("qe", d, c)])
                        TT("dve", keT[d][:, tsl], ps[:, bk, psl], ek_t[d][:, :], ALU.mult, [PS(bk), ("ek", d)], [("ke", d, c)])
                        TT("dve", ks[d][:, c, :], ps[:, bt, 0:128], ksf_t[d][:, :], ALU.mult, [PS(bt), ("ksf", d)], [("ks", d, c)])
            touched = set()
            for i in range(NT):
                for d in range(2):
                    c = i if d == 0 else NT - 1 - i
                    tsl = slice(c * 128, (c + 1) * 128)
                    first = (i == 0)
                    if i < NT - 1:
                        bkv = bank.next()
                        MM(ps[:, bkv, 0:256], ks[d][:, c, :], v_sb[:, c, :], True, True, [("ks", d, c), ("v", c)], [PS(bkv)])
                        if first:
                            CP("dve", S32[d][:, :], ps[:, bkv, 0:256], [PS(bkv)], [("S32", d)])
                        else:
                            STT(S32[d][:, :], S32[d][:, :], dec[d][:, c:c + 1], ps[:, bkv, 0:256], ALU.mult, ALU.add,
                                [PS(bkv), ("S32", d), ("dec", d)], [("S32", d)])
                        AC(Sbf[d][i % 2][:, :], S32[d][:, :], AF.Copy, [("S32", d)], [("Sbf", d, i % 2)])
                    ba = bank.next()
                    MM(ps[:, ba, 0:128], keT[d][:, tsl], qeT[d][:, tsl], True, True, [("ke", d, c), ("qe", d, c)], [PS(ba)])
                    TT("dve", at_t[d][:, :], ps[:, ba, 0:128], maskT[d], ALU.mult, [PS(ba), "cst"], [("at", d)])
                    bo = bank.next()
                    MM(ps[:, bo, 0:256], at_t[d][:, :], v_sb[:, c, :], True, first, [("at", d), ("v", c)], [PS(bo)], signal=first)
                    if not first:
                        MM(ps[:, bo, 0:256], qeT[d][:, tsl], Sbf[d][(i - 1) % 2][:, :], False, True,
                           [("qe", d, c), ("Sbf", d, (i - 1) % 2)], [PS(bo)])
                    if c not in touched:
                        touched.add(c)
                        AC(o_acc[:, c, :], ps[:, bo, 0:256], AF.Copy, [PS(bo)], [("oa", c)])
                    else:
                        TT("dve", o_acc[:, c, :], ps[:, bo, 0:256], o_acc[:, c, :], ALU.add, [PS(bo), ("oa", c)], [("oa", c)])
            def g_mm(c_):
                tsl_ = slice(c_ * 128, (c_ + 1) * 128)
                bg_ = bank.next()
                for k in range(16):
                    MM(ps[:, bg_, 0:256], hT[:, k, tsl_], wbuf[sgg][:, k, 0:256], k == 0, k == 15,
                       [("wb", sgg), hk(k)], [PS(bg_)], signal=(k == 15))
                return bg_
            bg_next = g_mm(0)
            for c in range(NT):
                tsl = slice(c * 128, (c + 1) * 128)
                bg = bg_next
                pi = c % 2
                gs_t, gsk, gob, gobk = gs2[pi], gs2k[pi], gob2[pi], ("ek", pi)
                smc = 8 + pi
                AC(gs_t[:, :], ps[:, bg, 0:256], AF.Silu, [PS(bg)], gsk)
                rms_stats_pool(o_acc[:, c, :], [("oa", c)], 256, smc, gob[:, :], gobk)
                STT(o_acc[:, c, :], o_acc[:, c, :], sm[:, smc:smc + 1], gnorm_b[:, :], ALU.mult, ALU.mult,
                    [("oa", c), ("sm", smc), "gnorm_b"], [("oa", c)])
                TT("pool", gob[:, :], o_acc[:, c, :], gs_t[:, :], ALU.mult, [("oa", c)] + gsk, [gobk])
                if c + 1 < NT:
                    bg_next = g_mm(c + 1)
                transpose_to(gob, [gobk], 2, lambda cc, tsl=tsl: qeT[cc][:, tsl], lambda cc: None,
                             lambda cc, c=c: [("qe", cc, c)])
            for cc in range(2):
                ch = 8 + h * 2 + cc
                row0 = (s * 16 + ch) * 128
                DMA("sp", mix_scr[row0:row0 + 128, :], qeT[cc][:, :], [("qe", cc, c) for c in range(NT)], [("mix", s, ch)], "d_mog%d" % cc)
        pre_wout = [load_w(wout_b, nb * 512, 512, ["wout"]) for nb in range(2)] if s > 0 else None
        SC.barrier(keep=WKEYS)

        pump(1000)
        if stop_after == "gla":
            break
        mixT = regA[:, 0:8192].rearrange("p (k t) -> p k t", k=16)
        hfT = regA[:, 8192:16384].rearrange("p (k t) -> p k t", k=16)
        aT = regA[:, 16384:24576].rearrange("p (k t) -> p k t", k=16)
        hf2 = [regA[:, 24576 + i * 2048:24576 + (i + 1) * 2048] for i in range(3)]
        rl_t = [regA[:, 30720 + i * 1024: 30720 + (i + 1) * 1024].bitcast(F32) for i in range(2)]

        def junk_of(ti):
            return regA[:, 16384 + ti * 2048:16384 + (ti + 1) * 2048], [("aT", 4 * ti + i) for i in range(4)]

        def load_mixT(blk_):
            src_ = mix_scr[s * 2048:(s + 1) * 2048, blk_ * 512:(blk_ + 1) * 512].rearrange("(k p) t -> p k t", p=128)
            DMA("sp", mixT, src_, [("mix", s, ch) for ch in range(16)], ["mixT"], "d_mixT")
        load_mixT(0)
        x1 = wview(0, [128, 4, 2048], F32)
        ft = wview(32768, [128, 4, 2048], F32)
        rl_i = Rot(2)
        if pre_wout is None:
            pre_wout = [load_w(wout_b, nb * 512, 512, ["wout"]) for nb in range(2)]
        for blk in range(NB):
            t0 = s * S + blk * 512
            for ti in range(4):
                DMA("pool", x1[:, ti, :], x[t0 + ti * 128: t0 + (ti + 1) * 128, :], [], [("x1", ti)], "d_x1%d" % ti)
            def s1(ti):
                jk, jkk = junk_of(ti)
                c1 = 16 + ti
                rms_stats(ft[:, ti, :], [("ft", ti)], 2048, c1, jk, jkk[0])
                STT(ft[:, ti, :], ft[:, ti, :], sm[:, c1:c1 + 1], wpost_b[:, :], ALU.mult, ALU.mult,
                    [("ft", ti), ("sm", c1), "wpost_b"] + jkk[1:], [("ft", ti)])
                TT("pool" if ti % 2 == 0 else "dve", x1[:, ti, :], x1[:, ti, :], ft[:, ti, :], ALU.add,
                   [("x1", ti), ("ft", ti)], [("x1", ti)])

            def s2a(ti):
                jk, jkk = junk_of(ti)
                c2 = 20 + ti
                rms_stats(x1[:, ti, :], [("x1", ti)], 2048, c2, jk, jkk[0])
                TS("dve", hf2[ti % 3], x1[:, ti, :], sm[:, c2:c2 + 1], None, ALU.mult, None,
                   [("x1", ti), ("sm", c2)], [("hf_t", ti % 3)])

            def s2b(ti):
                transpose_to(hf2[ti % 3], [("hf_t", ti % 3)], 16,
                             lambda c, ti=ti: hfT[:, c, ti * 128:(ti + 1) * 128],
                             lambda c: pp_t[:, 16 + c:17 + c],
                             lambda c: [("hfT", c)])
            for nb in range(4):
                sl = pre_wout[nb] if nb < 2 else load_w(wout_b, nb * 512, 512, ["wout"])
                for ti in range(4):
                    b = bank.next()
                    for k in range(16):
                        MM(ps[:, b, :], mixT[:, k, ti * 128:(ti + 1) * 128], wbuf[sl][:, k, :], k == 0, k == 15,
                           [("wb", sl), "mixT"], [PS(b)], signal=(k == 15))
                    dst = ft[:, ti, nb * 512:(nb + 1) * 512]
                    if (nb + ti) % 2 == 0:
                        AC(dst, ps[:, b, :], AF.Copy, [PS(b)], [("ft", ti)])
                    else:
                        CP("dve", dst, ps[:, b, :], [PS(b)], [("ft", ti)])
                    if nb == 3:
                        s1(ti)
                        if ti in (1, 2, 3):
                            s2a(ti - 1)
            s2b(0)
            s2b(1)
            s2b(2)
            s2a(3)
            s2b(3)
            for q in range(4):
                for jg in range(4):
                    sl = load_w(wup_b, (q * 16 + jg * 4) * 128, 512, ["wup"])
                    if q == 0 and jg == 2 and blk + 1 < NB:
                        load_mixT(blk + 1)
                    for jj in range(4):
                        j = jg * 4 + jj
                        b = bank.next()
                        for k in range(16):
                            MM(ps[:, b, :], wbuf[sl][:, k, jj * 128:(jj + 1) * 128], hfT[:, k, :], k == 0, k == 15,
                               [("wb", sl), ("hfT", k)], [PS(b)], signal=(k == 15))
                        ri = rl_i.next()
                        AC(rl_t[ri], ps[:, b, :], AF.Relu, [PS(b)], [("rl", ri)])
                        TT("pool" if j % 2 == 0 else "dve", aT[:, j, :], rl_t[ri], rl_t[ri], ALU.mult, [("rl", ri)], [("aT", j)])
                for nb in range(4):
                    sl = load_w(wdown_b, nb * 512, 512, ["wdown"], rows0=q * 2048)
                    for ti in range(4):
                        b = bank.next()
                        for j in range(16):
                            MM(ps[:, b, :], aT[:, j, ti * 128:(ti + 1) * 128], wbuf[sl][:, j, :], j == 0, j == 15,
                               [("wb", sl), ("aT", j)], [PS(b)], signal=(j == 15))
                        fsl = ft[:, ti, nb * 512:(nb + 1) * 512]
                        if q == 0:
                            AC(fsl, ps[:, b, :], AF.Copy, [PS(b)], [("ft", ti)])
                        else:
                            TT("dve", fsl, ps[:, b, :], fsl, ALU.add, [PS(b), ("ft", ti)], [("ft", ti)])
            if blk + 1 < NB:
                pre_wout = [load_w(wout_b, nb * 512, 512, ["wout"]) for nb in range(2)]
            for ti in range(4):
                jk, jkk = junk_of(ti)
                c3 = 24 + ti
                rms_stats(ft[:, ti, :], [("ft", ti)], 2048, c3, jk, jkk[0])
                STT(ft[:, ti, :], ft[:, ti, :], sm[:, c3:c3 + 1], wfpost_b[:, :], ALU.mult, ALU.mult,
                    [("ft", ti), ("sm", c3), "wfpost_b"] + jkk[1:], [("ft", ti)])
                TT("dve", ft[:, ti, :], ft[:, ti, :], x1[:, ti, :], ALU.add, [("x1", ti), ("ft", ti)], [("ft", ti)])
                DMA("pool", y[t0 + ti * 128: t0 + (ti + 1) * 128, :], ft[:, ti, :], [("ft", ti)], [("y", t0, ti)], "d_y%d" % ti)
        SC.barrier(keep=WKEYS)

    with nc.Block() as block:
        @block.tensor
        def _(e):
            SC.replay("pe", e)

        @block.scalar
        def _(e):
            SC.replay("act", e)

        @block.vector
        def _(e):
            SC.replay("dve", e)

        @block.gpsimd
        def _(e):
            SC.replay("pool", e)

        @block.sync
        def _(e):
            SC.replay("sp", e)
    return nc


def host_prep(inp):
    f = np.float32

    def colpack(v):
        v = np.asarray(v, f).reshape(-1, 128)
        return np.ascontiguousarray(v.T)
    cols = [colpack(inp["norm_mix_pre"][0]), colpack(inp["norm_ffn_pre"][0])]
    cw = np.asarray(inp["rg_conv_w"][0], f)
    for j in range(4):
        cols.append(colpack(cw[j]))
    cols.append(colpack(inp["rg_conv_b"][0]))
    for nm in ["rg_a_b_fwd", "rg_i_b_fwd", "rg_a_b_bwd", "rg_i_b_bwd", "rg_lambda_fwd", "rg_lambda_bwd"]:
        cols.append(colpack(inp[nm][0]))
    pp = np.ascontiguousarray(np.concatenate(cols, axis=1))
    assert pp.shape == (128, 120)
    bvec = np.concatenate([np.asarray(inp["norm_mix_post"][0], f), np.asarray(inp["norm_ffn_post"][0], f),
                           np.asarray(inp["gla_norm"][0], f)])[None, :]
    gkw = np.zeros((17, 1024), f)
    gkw[:16, :512] = inp["gla_gk_w_fwd"][0]
    gkw[16, :512] = inp["gla_gk_b_fwd"][0]
    gkw[:16, 512:] = inp["gla_gk_w_bwd"][0]
    gkw[16, 512:] = inp["gla_gk_b_bwd"][0]
    rgw = np.concatenate([np.asarray(inp[nm][0], f).reshape(1024, 256)
                          for nm in ["rg_a_w_fwd", "rg_i_w_fwd", "rg_a_w_bwd", "rg_i_w_bwd"]], axis=0)
    i_ = np.arange(128)
    tp, t = i_[:, None], i_[None, :]
    g = -1.0 / 16.0
    cst = np.concatenate([
        np.eye(128, dtype=f),
        (tp <= t).astype(f),
        (tp >= t).astype(f),
        (tp <= t).astype(f) * g,
        (tp >= t).astype(f) * g,
        (tp > t).astype(f) * g,
        (tp < t).astype(f) * g,
    ], axis=1).astype(f)
    return dict(
        w_in=np.ascontiguousarray(inp["w_in"][0], f), w_out=np.ascontiguousarray(inp["w_out"][0], f),
        w_up=np.ascontiguousarray(inp["w_up"][0], f), w_down=np.ascontiguousarray(inp["w_down"][0], f),
        rgw=np.ascontiguousarray(rgw), pp=pp, bvec=np.ascontiguousarray(bvec), gkw=gkw, cst=np.ascontiguousarray(cst))


_NC_CACHE = {}


def kernel(**inputs):
    S = 2048
    NSEQ = 3
    xp = np.asarray(inputs["x_prompt"], np.float32)
    xs = np.asarray(inputs["x_sample"], np.float32)
    allx = np.concatenate([xp, xs], axis=0)
    shared = host_prep(inputs)
    key = (S, NSEQ)
    if key not in _NC_CACHE:
        _NC_CACHE[key] = build(S, NSEQ)
    nc = _NC_CACHE[key]
    in_maps = []
    for c in range(NCORE):
        m = dict(shared)
        m["x"] = np.ascontiguousarray(allx[c * NSEQ:(c + 1) * NSEQ].reshape(NSEQ * S, D))
        in_maps.append(m)
    res = run_bass_kernel_spmd(nc, in_maps, core_ids=list(range(NCORE)))
    ys = np.concatenate([np.asarray(r["y"], np.float32).reshape(NSEQ, S, D) for r in res.results], axis=0)
    return ys[:8].copy(), ys[8:].copy()
```

```python
import numpy as np
import concourse.bass as bass
import concourse.mybir as mybir
from concourse.bass_utils import run_bass_kernel_spmd

F32 = mybir.dt.float32
BF16 = mybir.dt.bfloat16
AF = mybir.ActivationFunctionType
ALU = mybir.AluOpType

D = 2048
P_IN = 5152
DFF = 8192
NCORE = 8
EPS = 1e-6
ENG = ["pe", "act", "dve", "pool", "sp"]
import os
_EV = os.environ.get("EVSEL", "alt")


def EVAC_SEL(c):
    return {"alt": c % 2 == 0, "act": True, "dve": False}[_EV]


class Sched:
    def __init__(self, nc):
        self.nc = nc
        self.prog = {e: [] for e in ENG}
        self.semh = {}
        self.cnt = {}
        self.lastw = {}
        self.readers = {}
        self.known = {e: {} for e in ENG}
        for e in ["pe", "act", "dve", "pool"]:
            self._mk(e)

    def _mk(self, name):
        if name not in self.semh:
            self.semh[name] = self.nc.alloc_semaphore(name="s_" + name)
            self.cnt[name] = 0

    def op(self, eng, fn, reads=(), writes=(), signal=True, dsem=None):
        pr = [k for k in reads if isinstance(k, tuple) and k[0] == "ps"]
        if pr:
            reads = [k for k in reads if k not in pr]
            writes = list(writes) + pr
        deps = {}

        def add(tok):
            if tok is not None:
                deps[tok[0]] = max(deps.get(tok[0], 0), tok[1])

        for k in reads:
            add(self.lastw.get(k))
        for k in writes:
            add(self.lastw.get(k))
            for sn, v in self.readers.get(k, {}).items():
                add((sn, v))
        waits = []
        for sn, v in deps.items():
            if sn == "pe" and eng == "pe":
                continue
            if self.known[eng].get(sn, 0) >= v:
                continue
            self.known[eng][sn] = v
            waits.append((sn, v))
        if dsem is not None:
            self._mk(dsem)
            self.cnt[dsem] += 16
            tok = (dsem, self.cnt[dsem])
            rec = (waits, fn, dsem, 16)
        elif signal:
            self.cnt[eng] += 1
            tok = (eng, self.cnt[eng])
            rec = (waits, fn, eng, 1)
        else:
            tok = (eng, self.cnt[eng] + 1)
            rec = (waits, fn, None, 0)
        self.prog[eng].append(rec)
        for k in reads:
            r = self.readers.setdefault(k, {})
            r[tok[0]] = max(r.get(tok[0], 0), tok[1])
        for k in writes:
            self.lastw[k] = tok
            self.readers[k] = {}

    def barrier(self, keep=()):
        kept = {k: self.lastw[k] for k in keep if k in self.lastw}
        skip = {tok[0] for tok in kept.values()}
        for e in ENG:
            waits = []
            for sn, c in self.cnt.items():
                if c == 0 or (sn == "pe" and e == "pe") or sn in skip:
                    continue
                if self.known[e].get(sn, 0) >= c:
                    continue
                self.known[e][sn] = c
                waits.append((sn, c))
            if waits:
                self.prog[e].append((waits, None, None, 0))
        self.lastw = dict(kept)
        self.readers = {}

    def replay(self, eng, e):
        for waits, fn, sn, inc in self.prog[eng]:
            for wsn, v in waits:
                e.wait_ge(self.semh[wsn], v)
            if fn is None:
                continue
            ins = fn(e)
            if sn is not None:
                ins.then_inc(self.semh[sn], inc)


def build(S, NSEQ, stop_after=None):
    nc = bass.Bass("TRN2", target_bir_lowering=False)
    NT = S // 128
    NB = S // 512
    T = NSEQ * S
    dt = nc.dram_tensor
    x = dt("x", [T, D], F32, kind="ExternalInput").ap()
    w_in = dt("w_in", [D, P_IN], F32, kind="ExternalInput").ap()
    w_out = dt("w_out", [D, D], F32, kind="ExternalInput").ap()
    w_up = dt("w_up", [D, DFF], F32, kind="ExternalInput").ap()
    w_down = dt("w_down", [DFF, D], F32, kind="ExternalInput").ap()
    rgw = dt("rgw", [4096, 256], F32, kind="ExternalInput").ap()
    pp = dt("pp", [128, 120], F32, kind="ExternalInput").ap()
    bvec = dt("bvec", [1, 4352], F32, kind="ExternalInput").ap()
    gkw = dt("gkw", [17, 1024], F32, kind="ExternalInput").ap()
    cst = dt("cst", [128, 7 * 128], F32, kind="ExternalInput").ap()
    y = dt("y", [T, D], F32, kind="ExternalOutput").ap()
    win_b = dt("win_b", [D, P_IN], BF16, kind="Internal").ap()
    wout_b = dt("wout_b", [D, D], BF16, kind="Internal").ap()
    wup_b = dt("wup_b", [D, DFF], BF16, kind="Internal").ap()
    wdown_b = dt("wdown_b", [DFF, D], BF16, kind="Internal").ap()
    rgw_b = dt("rgw_b", [4096, 256], BF16, kind="Internal").ap()
    mix_scr = dt("mix_scr", [NSEQ * 16 * 128, S], BF16, kind="Internal").ap()

    sb = nc.alloc_sbuf_tensor
    pp_t = sb("pp_t", [128, 120], F32)
    ppx = sb("ppx", [128, 16], F32)
    ppx2 = sb("ppx2", [128, 16], F32)
    cst_t = sb("cst_t", [128, 7 * 128], F32)
    idb = sb("idb", [128, 128], BF16)
    bv_t = sb("bv_t", [1, 512], F32)
    ones1 = sb("ones1", [1, 128], F32)
    wpost_b = sb("wpost_b", [128, 2048], F32)
    wfpost_b = sb("wfpost_b", [128, 2048], F32)
    gnorm_b = sb("gnorm_b", [128, 256], F32)
    gkw_t = sb("gkw_t", [64, 512], BF16)
    eps_t = sb("eps_t", [128, 1], F32)
    one_t = sb("one_t", [128, 1], F32)
    mhalf_t = sb("mhalf_t", [128, 1], F32)
    sm = sb("sm", [128, 32], F32)
    wbuf = [sb("wbuf%d" % i, [128, 16, 512], BF16) for i in range(3)]
    rgw_t = sb("rgw_t", [128, 4, 2, 256], BF16)
    wlr_t = sb("wlr_t", [128, 16, 32], BF16)
    regA = sb("regA", [128, 16 * 2048], BF16)
    WBYTES = 64 * 1024
    regW = sb("regW", [128, WBYTES // 4], F32)
    ps = nc.alloc_psum_tensor("ps", [128, 8, 512], F32)

    SC = Sched(nc)
    op = SC.op

    def MM(out, lhsT, rhs, start, stop, reads, writes, signal=True):
        op("pe", lambda e: e.matmul(out, lhsT, rhs, start=start, stop=stop), reads, writes, signal)

    def TR(out, in_, reads, writes, signal=True):
        op("pe", lambda e: e.transpose(out, in_, idb[:]), reads, writes, signal)

    def AC(out, in_, func, reads, writes, bias=None, scale=1.0, accum=None):
        kw = {}
        if bias is not None:
            kw["bias"] = bias
        if accum is not None:
            kw["accum_out"] = accum
        op("act", lambda e: e.activation(out=out, in_=in_, func=func, scale=scale, **kw), reads, writes)

    def TS(eng, out, in0, s1, s2, op0, op1, reads, writes):
        if op1 is None:
            op(eng, lambda e: e.tensor_scalar(out=out, in0=in0, scalar1=s1, scalar2=None, op0=op0), reads, writes)
        else:
            op(eng, lambda e: e.tensor_scalar(out=out, in0=in0, scalar1=s1, scalar2=s2, op0=op0, op1=op1), reads, writes)

    def STT(out, in0, scalar, in1, op0, op1, reads, writes):
        op("dve", lambda e: e.scalar_tensor_tensor(out=out, in0=in0, scalar=scalar, in1=in1, op0=op0, op1=op1), reads, writes)

    deferred = []
    pump_ctr = [0]

    def pump(n=1):
        for _ in range(n):
            if deferred:
                d_, s_, key = deferred.pop(0)
                DMA("pool", d_, s_, [], [key, ("castslot", cast_i[0] % 3)], "c_" + key)
                cast_i[0] += 1

    def TT(eng, out, in0, in1, aop, reads, writes):
        op(eng, lambda e: e.tensor_tensor(out=out, in0=in0, in1=in1, op=aop), reads, writes)
        if eng == "pool":
            pump_ctr[0] += 1
            if pump_ctr[0] % 3 == 0:
                pump()

    def CP(eng, out, in_, reads, writes):
        op(eng, lambda e: e.tensor_copy(out=out, in_=in_), reads, writes)

    def MS(eng, ap, val, writes):
        op(eng, lambda e: e.memset(ap, val), [], writes)

    def SCAN(out, d0, d1, reads, writes):
        op("dve", lambda e: e.tensor_tensor_scan(out=out, data0=d0, data1=d1, initial=0.0, op0=ALU.mult, op1=ALU.add), reads, writes)

    def RECIP(out, in_, reads, writes):
        op("dve", lambda e: e.reciprocal(out=out, in_=in_), reads, writes)

    def DMA(eng, out, in_, reads, writes, dsem):
        op(eng, lambda e: e.dma_start(out=out, in_=in_), reads, writes, dsem=dsem)

    def PS(b):
        return ("ps", b)

    class Rot:
        def __init__(self, n, base=0):
            self.n, self.i, self.base = n, 0, base

        def next(self):
            v = self.base + self.i % self.n
            self.i += 1
            return v

    bank = Rot(8)
    wslot = Rot(3)

    def wview(off_bytes, shape, dtype):
        esz = 4 if dtype == F32 else 2
        n = 1
        for s_ in shape[1:]:
            n *= s_
        assert off_bytes % 4 == 0 and (n * esz) % 4 == 0
        assert off_bytes + n * esz <= WBYTES, (off_bytes, n * esz)
        v = regW[:, off_bytes // 4:(off_bytes + n * esz) // 4]
        if dtype != F32:
            v = v.bitcast(dtype)
        if len(shape) == 3:
            v = v.rearrange("p (a b) -> p a b", a=shape[1])
        return v


    cast_i = [0]

    def cast_w(src, dst, rows, cols, key, defer=False):
        piece = cols
        while piece > 2048:
            piece //= 2
        npc = cols // piece
        R = 256
        for r0 in range(0, rows, R):
            s_ = src[r0:r0 + R, :].rearrange("r (a b) -> r a b", a=npc)
            d_ = dst[r0:r0 + R, :].rearrange("r (a b) -> r a b", a=npc)
            if defer:
                deferred.append((d_, s_, key))
                continue
            DMA("pool", d_, s_, [], [key, ("castslot", cast_i[0] % 3)], "c_" + key)
            cast_i[0] += 1

    cast_w(w_in, win_b, D, P_IN, "win")
    cast_w(rgw, rgw_b, 4096, 256, "rgwb")
    cast_w(w_out, wout_b, D, D, "wout", defer=True)
    cast_w(w_up, wup_b, D, DFF, "wup", defer=True)
    cast_w(w_down, wdown_b, DFF, D, "wdown", defer=True)

    DMA("sp", pp_t[:], pp[:, :], [], ["pp"], "d_pp")
    DMA("sp", cst_t[:], cst[:, :], [], ["cst"], "d_cst")
    gkw_f = wview(0, [128, 512], F32)
    DMA("sp", gkw_f[0:17, :], gkw[:, 0:512], [], ["gkwf"], "d_gk")
    DMA("sp", gkw_f[32:49, :], gkw[:, 512:1024], [], ["gkwf"], "d_gk")
    MS("dve", eps_t[:], EPS, ["eps"])
    MS("dve", one_t[:], 1.0, ["one"])
    MS("dve", mhalf_t[:], -0.5, ["mhalf"])
    MS("dve", ones1[:], 1.0, ["ones1"])
    CP("dve", idb[:], cst_t[:, 0:128], ["cst"], ["idb"])
    CP("dve", gkw_t[0:17, :], gkw_f[0:17, :], ["gkwf"], ["gkw"])
    CP("dve", gkw_t[32:49, :], gkw_f[32:49, :], ["gkwf"], ["gkw"])
    AC(ppx[:], pp_t[:, 104:120], AF.Exp, ["pp"], ["ppx"], scale=-1.0)
    AC(ppx[:], ppx[:], AF.Ln, ["ppx", "one"], ["ppx"], bias=one_t[:, 0:1])
    TS("dve", ppx[:], ppx[:], -8.0, None, ALU.mult, None, ["ppx"], ["ppx"])
    TS("dve", ppx2[:], ppx[:], 2.0, None, ALU.mult, None, ["ppx"], ["ppx2"])
    for (dst, off, n) in [(wpost_b, 0, 2048), (wfpost_b, 2048, 2048), (gnorm_b, 4096, 256)]:
        DMA("sp", dst[:, :], bvec[0:1, off:off + n].broadcast_to([128, n]), [], [dst.name], "d_bv")
    maskT = [cst_t[:, 128:256], cst_t[:, 256:384]]
    Ucum = [cst_t[:, 384:512], cst_t[:, 512:640]]
    Mks = [cst_t[:, 640:768], cst_t[:, 768:896]]

    WKEYS = ["win", "rgwb", "wout", "wup", "wdown"]
    SC.barrier(keep=WKEYS)
    hT = regA[:].rearrange("p (k t) -> p k t", k=16)[:, :, 0:S]

    def hk(k):
        return ("hT", k)

    def rms_stats(src_ap, src_keys, n, col, junk, junk_key):
        cs = sm[:, col:col + 1]
        AC(junk, src_ap, AF.Square, src_keys, [junk_key, ("sm", col)], accum=cs)
        AC(cs, cs, AF.Sqrt, [("sm", col), "eps"], [("sm", col)], bias=eps_t[:, 0:1], scale=1.0 / n)
        RECIP(cs, cs, [("sm", col)], [("sm", col)])

    def rms_stats_pool(src_ap, src_keys, n, col, junk, junk_key):
        cs = sm[:, col:col + 1]
        AC(junk, src_ap, AF.Square, src_keys, [junk_key, ("sm", col)], accum=cs)
        TS("pool", cs, cs, 1.0 / n, EPS, ALU.mult, ALU.add, [("sm", col)], [("sm", col)])
        TT("pool", cs, cs, mhalf_t[:, 0:1], ALU.pow, [("sm", col), "mhalf"], [("sm", col)])

    def transpose_to(src_bf, src_keys, nchunk, dst_fn, scale_fn, dst_keys_fn, force=None):
        for g0 in range(0, nchunk, 4):
            b = bank.next()
            pb = ps[:, b, :].bitcast(BF16)
            gn = min(4, nchunk - g0)
            for i in range(gn):
                c = g0 + i
                TR(pb[:, i * 128:(i + 1) * 128], src_bf[:, c * 128:(c + 1) * 128], list(src_keys) + ["idb"], [PS(b)], signal=(i == gn - 1))
            for i in range(gn):
                c = g0 + i
                sc = scale_fn(c)
                src = pb[:, i * 128:(i + 1) * 128]
                if (force == "act") or (force is None and (g0 // 4) % 2 == 0):
                    if sc is None:
                        AC(dst_fn(c), src, AF.Copy, [PS(b)], dst_keys_fn(c))
                    else:
                        AC(dst_fn(c), src, AF.Copy, [PS(b), "pp"], dst_keys_fn(c), scale=sc)
                else:
                    if sc is None:
                        CP("dve", dst_fn(c), src, [PS(b)], dst_keys_fn(c))
                    else:
                        TS("dve", dst_fn(c), src, sc, None, ALU.mult, None, [PS(b), "pp"], dst_keys_fn(c))

    def load_w(src2d, col0, ncols, key_reads, slot=None, sub0=0, rows0=0):
        if slot is None:
            slot = wslot.next()
        src = src2d[rows0:rows0 + 2048, col0:col0 + ncols].rearrange("(k p) c -> p k c", p=128)
        DMA("sp", wbuf[slot][:, :, sub0:sub0 + ncols], src, key_reads, [("wb", slot)], "d_wb%d" % slot)
        return slot

    def proj_fm(slot, sub0, m, evac):
        for blk in range(NB):
            b = bank.next()
            for k in range(16):
                MM(ps[0:m, b, :], wbuf[slot][:, k, sub0:sub0 + m], hT[:, k, blk * 512:(blk + 1) * 512],
                   k == 0, k == 15, [("wb", slot), hk(k)], [PS(b)], signal=(k == 15))
            evac(blk, b)

    for s in range(NSEQ if stop_after != "setup" else 0):
        xt = [wview(i * 8192, [128, 2048], F32) for i in range(2)]
        xs2 = [wview(16384 + i * 4096, [128, 2048], BF16) for i in range(2)]
        jk1 = [wview(24576 + i * 4096, [128, 2048], BF16) for i in range(2)]
        for tt in range(NT):
            r0 = s * S + tt * 128
            xb = tt % 2
            DMA("sp", xt[xb], x[r0:r0 + 128, :], [], [("xt", xb)], "d_xt%d" % xb)
            if stop_after == "p1x":
                continue
            xs, xsk = xs2[xb], ("xs", xb)
            rms_stats(xt[xb], [("xt", xb)], 2048, xb, jk1[xb], ("jk1", xb))
            if stop_after == "p1a":
                continue
            TS("dve", xs, xt[xb], sm[:, xb:xb + 1], None, ALU.mult, None, [("xt", xb), ("sm", xb)], [xsk])
            if stop_after == "p1b":
                continue
            transpose_to(xs, [xsk], 16,
                         lambda c, tt=tt: hT[:, c, tt * 128:(tt + 1) * 128],
                         lambda c: pp_t[:, c:c + 1],
                         lambda c: [hk(c)])
        def rg_loads(n_):
            sx_ = load_w(win_b, n_ * 256, 256, ["win"])
            sg_ = load_w(win_b, 1024 + n_ * 256, 256, ["win"])
            for mi_ in range(4):
                src_ = rgw_b[mi_ * 1024 + n_ * 256: mi_ * 1024 + (n_ + 1) * 256, :].rearrange("(c p) o -> p c o", p=128)
                DMA("sp", rgw_t[:, mi_, :, :], src_, ["rgwb"], ["rgw_t"], "d_rgw")
            return sx_, sg_
        rg_pre = rg_loads(0)
        SC.barrier(keep=WKEYS)

        if stop_after in ("p1", "p1x", "p1a", "p1b"):
            break
        SP4 = (S + 4) * 4
        Bf = [wview(i * SP4, [128, S + 4], F32) for i in range(6)]
        off = 6 * SP4
        XB = [wview(off + i * S * 2, [128, S], BF16) for i in range(2)]
        off += 2 * S * 2
        MO = [wview(off, [128, S], BF16)]
        off += S * 2
        assert off <= WBYTES, off
        mo_i = Rot(1)
        XR = [0, 1]
        XC = [2, 3]
        for n in range(4):
            sx, sg = rg_pre if n == 0 else rg_loads(n)
            for cc in range(2):
                ch = n * 2 + cc
                xr = Bf[XR[cc]]
                kxr = ("B", XR[cc])
                MS("pool", xr[:, 0:2], 0.0, [kxr])
                MS("pool", xr[:, S + 2:S + 4], 0.0, [kxr])

                def ev_x(blk, b, xr=xr, kxr=kxr):
                    dst = xr[:, 2 + blk * 512: 2 + (blk + 1) * 512]
                    if blk % 2 == 0:
                        AC(dst, ps[:, b, :], AF.Copy, [PS(b)], [kxr])
                    else:
                        CP("dve", dst, ps[:, b, :], [PS(b)], [kxr])
                proj_fm(sx, cc * 128, 128, ev_x)
                xc = Bf[XC[cc]]
                kxc = ("B", XC[cc])
                TS("dve", xc[:, 0:S], xr[:, 0:S], pp_t[:, 32 + ch:33 + ch], pp_t[:, 64 + ch:65 + ch], ALU.mult, ALU.add,
                   [kxr, "pp"], [kxc])
                for j in range(1, 4):
                    STT(xc[:, 0:S], xr[:, j:j + S], pp_t[:, 32 + j * 8 + ch:33 + j * 8 + ch], xc[:, 0:S], ALU.mult, ALU.add,
                        [kxr, kxc, "pp"], [kxc])
                CP("pool", XB[cc][:, :], xc[:, 0:S], [kxc], [("XB", cc)])
            for oc in range(2):
                ch = n * 2 + oc
                xc = Bf[XC[oc]]
                kxc = ("B", XC[oc])
                bank_u = Rot(4, 0)
                for d in range(2):
                    for gi in range(2):
                        mi = 2 * d + gi
                        bcol = 72 + 16 * d + 8 * gi + ch
                        gdst = Bf[1] if gi == 0 else Bf[4]
                        gkey = ("B", 1) if gi == 0 else ("B", 4)
                        for blk in range(NB):
                            b = bank_u.next()
                            for cc in range(2):
                                MM(ps[:, b, :], rgw_t[:, mi, cc, oc * 128:(oc + 1) * 128], XB[cc][:, blk * 512:(blk + 1) * 512],
                                   cc == 0, cc == 1, ["rgw_t", ("XB", cc)], [PS(b)], signal=(cc == 1))
                            AC(gdst[:, blk * 512:(blk + 1) * 512], ps[:, b, :], AF.Sigmoid, [PS(b), "pp"], [gkey],
                               bias=pp_t[:, bcol:bcol + 1])
                    if d == 0:
                        for blk in range(NB):
                            bgp = 4 + blk
                            for k in range(16):
                                MM(ps[:, bgp, :], wbuf[sg][:, k, oc * 128:(oc + 1) * 128], hT[:, k, blk * 512:(blk + 1) * 512],
                                   k == 0, k == 15, [("wb", sg), hk(k)], [PS(bgp)], signal=(k == 15))
                    a_, u_, tmp = Bf[1][:, 0:S], Bf[4][:, 0:S], Bf[0][:, 0:S]
                    ka, ku, kt = ("B", 1), ("B", 4), ("B", 0)
                    ccol = 8 * d + ch
                    TT("pool", u_, u_, xc[:, 0:S], ALU.mult, [ku, kxc], [ku])
                    AC(tmp, a_, AF.Exp, [ka, "ppx2"], [kt], scale=ppx2[:, ccol:ccol + 1])
                    AC(a_, a_, AF.Exp, [ka, "ppx"], [ka], scale=ppx[:, ccol:ccol + 1])
                    AC(tmp, tmp, AF.Sqrt, [kt, "one"], [kt], bias=one_t[:, 0:1], scale=-1.0)
                    TT("dve", u_, u_, tmp, ALU.mult, [ku, kt], [ku])
                    if d == 0:
                        SCAN(Bf[5][:, 0:S], a_, u_, [ka, ku], [("B", 5)])
                    else:
                        SCAN(Bf[0][:, 0:S][:, ::-1], a_[:, ::-1], u_[:, ::-1], [ka, ku], [("B", 0)])
                TT("dve", Bf[5][:, 0:S], Bf[5][:, 0:S], Bf[0][:, 0:S], ALU.add, [("B", 5), ("B", 0)], [("B", 5)])
                for blk in range(NB):
                    AC(Bf[1][:, blk * 512:(blk + 1) * 512], ps[:, 4 + blk, :], AF.Gelu, [PS(4 + blk)], [("B", 1)])
                mi_ = mo_i.next()
                TT("dve", MO[mi_][:, :], Bf[5][:, 0:S], Bf[1][:, 0:S], ALU.mult, [("B", 5), ("B", 1)], [("MO", mi_)])
                row0 = (s * 16 + ch) * 128
                DMA("sp", mix_scr[row0:row0 + 128, :], MO[mi_][:, :], [("MO", mi_)], [("mix", s, ch)], "d_mo%d" % mi_)
        def gla_loads(h_):
            sqk_ = load_w(win_b, 2048 + h_ * 128, 128, ["win"])
            load_w(win_b, 2560 + h_ * 128, 128, ["win"], slot=sqk_, sub0=128)
            skv_ = load_w(win_b, 2560 + h_ * 128, 128, ["win"])
            load_w(win_b, 3072 + h_ * 256, 256, ["win"], slot=skv_, sub0=128)
            sgg_ = load_w(win_b, 4096 + h_ * 256, 256, ["win"])
            return sqk_, skv_, sgg_
        DMA("sp", wlr_t[:], win_b[:, 5120:5152].rearrange("(k p) c -> p k c", p=128), ["win"], ["wlr"], "d_wlr")
        gla_pre = gla_loads(0)
        SC.barrier(keep=WKEYS)

        if stop_after == "rg":
            break
        o = [0]

        def carve(shape, dtype):
            n_ = 1
            for s_ in shape[1:]:
                n_ *= s_
            v = wview(o[0], shape, dtype)
            o[0] += ((n_ * (4 if dtype == F32 else 2) + 3) // 4) * 4
            return v
        lrT_all = carve([128, S], BF16)
        qeT = [carve([128, S], BF16) for _ in range(2)]
        keT = [carve([128, S], BF16) for _ in range(2)]
        ks = [carve([128, NT, 128], BF16) for _ in range(2)]
        v_sb = carve([128, NT, 256], BF16)
        o_acc = carve([128, NT, 256], F32)
        dec = [carve([128, NT], F32) for _ in range(2)]
        lpair = carve([128, 256], F32)
        eqpair = carve([128, 256], F32)
        l_t = [lpair[:, d * 128:(d + 1) * 128] for d in range(2)]
        eq_t = [eqpair[:, d * 128:(d + 1) * 128] for d in range(2)]
        ek_t = [carve([128, 128], F32) for _ in range(2)]
        gs2 = [lpair, eqpair]
        gs2k = [[("l", 0), ("l", 1)], [("eq", 0), ("eq", 1)]]
        gob2 = [ek_t[i].bitcast(BF16) for i in range(2)]
        ksf_t = [carve([128, 128], F32) for _ in range(2)]
        at_t = [carve([128, 128], BF16) for _ in range(2)]
        S32 = [carve([128, 256], F32) for _ in range(2)]
        Sbf = [[carve([128, 256], BF16) for _ in range(2)] for _ in range(2)]
        assert o[0] <= WBYTES, o[0]

        for d in range(2):
            MS("pool", lrT_all[32 * d:32 * d + 32, :], 1.0, [("lrT", d)])
            for blk in range(NB):
                b = bank.next()
                for k in range(16):
                    MM(ps[0:16, b, :], wlr_t[:, k, d * 16:(d + 1) * 16], hT[:, k, blk * 512:(blk + 1) * 512],
                       k == 0, k == 15, ["wlr", hk(k)], [PS(b)], signal=(k == 15))
                AC(lrT_all[32 * d:32 * d + 16, blk * 512:(blk + 1) * 512], ps[0:16, b, :], AF.Copy, [PS(b)], [("lrT", d)])

        for h in range(4):
            sqk, skv, sgg = gla_pre if h == 0 else gla_loads(h)
            def kv_tok(c_):
                bt_ = 4 + c_ % 2
                tsl_ = slice(c_ * 128, (c_ + 1) * 128)
                for k in range(16):
                    MM(ps[:, bt_, 0:384], hT[:, k, tsl_], wbuf[skv][:, k, 0:384], k == 0, k == 15,
                       [("wb", skv), hk(k)], [PS(bt_)], signal=(k == 15))
            for blk in range(NB):
                bq, bk = (0, 1) if blk % 2 == 0 else (2, 3)
                for (bb, sub) in ((bq, 0), (bk, 128)):
                    for k in range(16):
                        MM(ps[:, bb, :], wbuf[sqk][:, k, sub:sub + 128], hT[:, k, blk * 512:(blk + 1) * 512],
                           k == 0, k == 15, [("wb", sqk), hk(k)], [PS(bb)], signal=(k == 15))
                if blk == 0:
                    kv_tok(0)
                for ti in range(4):
                    c = blk * 4 + ti
                    tsl = slice(c * 128, (c + 1) * 128)
                    psl = slice(ti * 128, (ti + 1) * 128)
                    bt = 4 + c % 2
                    bxc = [6, 7]
                    for d in range(2):
                        MM(ps[:, bxc[d], 256:384], lrT_all[32 * d:32 * d + 17, tsl], gkw_t[32 * d:32 * d + 17, h * 128:(h + 1) * 128],
                           True, True, [("lrT", d), "gkw"], [PS(bxc[d])])
                    if c + 1 < NT:
                        kv_tok(c + 1)
                    for d in range(2):
                        AC(l_t[d][:, :], ps[:, bxc[d], 256:384], AF.Exp, [PS(bxc[d])], [("l", d)], scale=-1.0)
                    for d in range(2):
                        AC(l_t[d][:, :], l_t[d][:, :], AF.Ln, [("l", d), "one"], [("l", d)], bias=one_t[:, 0:1])
                    for d in range(2):
                        bc = bxc[d]
                        MM(ps[:, bc, 0:128], l_t[d][:, :], Ucum[d], True, True, [("l", d), "cst"], [PS(bc)], signal=False)
                        MM(ps[:, bc, 128:256], Mks[d], l_t[d][:, :], True, True, [("l", d), "cst"], [PS(bc)])
                    AC(v_sb[:, c, :], ps[:, bt, 128:384], AF.Copy, [PS(bt)], [("v", c)])
                    for d in range(2):
                        bc = bxc[d]
                        AC(eq_t[d][:, :], ps[:, bc, 0:128], AF.Exp, [PS(bc)], [("eq", d)])
                        AC(ek_t[d][:, :], ps[:, bc, 0:128], AF.Exp, [PS(bc)], [("ek", d)], scale=-1.0)
                        lastc = 127 if d == 0 else 0
                        AC(dec[d][:, c:c + 1], ps[:, bc, lastc:lastc + 1], AF.Exp, [PS(bc)], [("dec", d)])
                        AC(ksf_t[d][:, :], ps[:, bc, 128:256], AF.Exp, [PS(bc)], [("ksf", d)])
                    for d in range(2):
                        STT(qeT[d][:, tsl], ps[:, bq, psl], 128.0 ** -0.5, eq_t[d][:, :], ALU.mult, ALU.mult,
                            [PS(bq), ("eq", d)], [("qe", d, c)])
                        TT("dve", keT[d][:, tsl], ps[:, bk, psl], ek_t[d][:, :], ALU.mult, [PS(bk), ("ek", d)], [("ke", d, c)])
                        TT("dve", ks[d][:, c, :], ps[:, bt, 0:128], ksf_t[d][:, :], ALU.mult, [PS(bt), ("ksf", d)], [("ks", d, c)])
            touched = set()
            for i in range(NT):
                for d in range(2):
                    c = i if d == 0 else NT - 1 - i
                    tsl = slice(c * 128, (c + 1) * 128)
                    first = (i == 0)
                    if i < NT - 1:
                        bkv = bank.next()
                        MM(ps[:, bkv, 0:256], ks[d][:, c, :], v_sb[:, c, :], True, True, [("ks", d, c), ("v", c)], [PS(bkv)])
                        if first:
                            CP("dve", S32[d][:, :], ps[:, bkv, 0:256], [PS(bkv)], [("S32", d)])
                        else:
                            STT(S32[d][:, :], S32[d][:, :], dec[d][:, c:c + 1], ps[:, bkv, 0:256], ALU.mult, ALU.add,
                                [PS(bkv), ("S32", d), ("dec", d)], [("S32", d)])
                        AC(Sbf[d][i % 2][:, :], S32[d][:, :], AF.Copy, [("S32", d)], [("Sbf", d, i % 2)])
                    ba = bank.next()
                    MM(ps[:, ba, 0:128], keT[d][:, tsl], qeT[d][:, tsl], True, True, [("ke", d, c), ("qe", d, c)], [PS(ba)])
                    TT("dve", at_t[d][:, :], ps[:, ba, 0:128], maskT[d], ALU.mult, [PS(ba), "cst"], [("at", d)])
                    bo = bank.next()
                    MM(ps[:, bo, 0:256], at_t[d][:, :], v_sb[:, c, :], True, first, [("at", d), ("v", c)], [PS(bo)], signal=first)
                    if not first:
                        MM(ps[:, bo, 0:256], qeT[d][:, tsl], Sbf[d][(i - 1) % 2][:, :], False, True,
                           [("qe", d, c), ("Sbf", d, (i - 1) % 2)], [PS(bo)])
                    if c not in touched:
                        touched.add(c)
                        AC(o_acc[:, c, :], ps[:, bo, 0:256], AF.Copy, [PS(bo)], [("oa", c)])
                    else:
                        TT("dve", o_acc[:, c, :], ps[:, bo, 0:256], o_acc[:, c, :], ALU.add, [PS(bo), ("oa", c)], [("oa", c)])
            def g_mm(c_):
                tsl_ = slice(c_ * 128, (c_ + 1) * 128)
                bg_ = bank.next()
                for k in range(16):
                    MM(ps[:, bg_, 0:256], hT[:, k, tsl_], wbuf[sgg][:, k, 0:256], k == 0, k == 15,
                       [("wb", sgg), hk(k)], [PS(bg_)], signal=(k == 15))
                return bg_
            bg_next = g_mm(0)
            for c in range(NT):
                tsl = slice(c * 128, (c + 1) * 128)
                bg = bg_next
                pi = c % 2
                gs_t, gsk, gob, gobk = gs2[pi], gs2k[pi], gob2[pi], ("ek", pi)
                smc = 8 + pi
                AC(gs_t[:, :], ps[:, bg, 0:256], AF.Silu, [PS(bg)], gsk)
                rms_stats_pool(o_acc[:, c, :], [("oa", c)], 256, smc, gob[:, :], gobk)
                STT(o_acc[:, c, :], o_acc[:, c, :], sm[:, smc:smc + 1], gnorm_b[:, :], ALU.mult, ALU.mult,
                    [("oa", c), ("sm", smc), "gnorm_b"], [("oa", c)])
                TT("dve", gob[:, :], o_acc[:, c, :], gs_t[:, :], ALU.mult, [("oa", c)] + gsk, [gobk])
                if c + 1 < NT:
                    bg_next = g_mm(c + 1)
                transpose_to(gob, [gobk], 2, lambda cc, tsl=tsl: qeT[cc][:, tsl], lambda cc: None,
                             lambda cc, c=c: [("qe", cc, c)], force="dve")
            for cc in range(2):
                ch = 8 + h * 2 + cc
                row0 = (s * 16 + ch) * 128
                DMA("sp", mix_scr[row0:row0 + 128, :], qeT[cc][:, :], [("qe", cc, c) for c in range(NT)], [("mix", s, ch)], "d_mog%d" % cc)
        pre_wout = [load_w(wout_b, nb * 512, 512, ["wout"]) for nb in range(2)] if s > 0 else None
        SC.barrier(keep=WKEYS)

        pump(1000)
        if stop_after == "gla":
            break
        mixT = regA[:, 0:8192].rearrange("p (k t) -> p k t", k=16)
        hfT = regA[:, 8192:16384].rearrange("p (k t) -> p k t", k=16)
        aT = regA[:, 16384:24576].rearrange("p (k t) -> p k t", k=16)
        hf2 = [regA[:, 24576 + i * 2048:24576 + (i + 1) * 2048] for i in range(3)]
        rl_t = [regA[:, 30720 + i * 1024: 30720 + (i + 1) * 1024].bitcast(F32) for i in range(2)]

        def junk_of(ti):
            return regA[:, 16384 + ti * 2048:16384 + (ti + 1) * 2048], [("aT", 4 * ti + i) for i in range(4)]

        def load_mixT(blk_):
            src_ = mix_scr[s * 2048:(s + 1) * 2048, blk_ * 512:(blk_ + 1) * 512].rearrange("(k p) t -> p k t", p=128)
            DMA("sp", mixT, src_, [("mix", s, ch) for ch in range(16)], ["mixT"], "d_mixT")
        load_mixT(0)
        x1 = wview(0, [128, 4, 2048], F32)
        ft = wview(32768, [128, 4, 2048], F32)
        rl_i = Rot(2)
        if pre_wout is None:
            pre_wout = [load_w(wout_b, nb * 512, 512, ["wout"]) for nb in range(2)]
        for blk in range(NB):
            t0 = s * S + blk * 512
            for ti in range(4):
                DMA("pool", x1[:, ti, :], x[t0 + ti * 128: t0 + (ti + 1) * 128, :], [], [("x1", ti)], "d_x1%d" % ti)
            def s1(ti):
                jk, jkk = junk_of(ti)
                c1 = 16 + ti
                rms_stats(ft[:, ti, :], [("ft", ti)], 2048, c1, jk, jkk[0])
                STT(ft[:, ti, :], ft[:, ti, :], sm[:, c1:c1 + 1], wpost_b[:, :], ALU.mult, ALU.mult,
                    [("ft", ti), ("sm", c1), "wpost_b"] + jkk[1:], [("ft", ti)])
                TT("pool" if ti % 2 == 0 else "dve", x1[:, ti, :], x1[:, ti, :], ft[:, ti, :], ALU.add,
                   [("x1", ti), ("ft", ti)], [("x1", ti)])

            def s2a(ti):
                jk, jkk = junk_of(ti)
                c2 = 20 + ti
                rms_stats(x1[:, ti, :], [("x1", ti)], 2048, c2, jk, jkk[0])
                TS("dve", hf2[ti % 3], x1[:, ti, :], sm[:, c2:c2 + 1], None, ALU.mult, None,
                   [("x1", ti), ("sm", c2)], [("hf_t", ti % 3)])

            def s2b(ti):
                transpose_to(hf2[ti % 3], [("hf_t", ti % 3)], 16,
                             lambda c, ti=ti: hfT[:, c, ti * 128:(ti + 1) * 128],
                             lambda c: pp_t[:, 16 + c:17 + c],
                             lambda c: [("hfT", c)])
            for nb in range(4):
                sl = pre_wout[nb] if nb < 2 else load_w(wout_b, nb * 512, 512, ["wout"])
                for ti in range(4):
                    b = bank.next()
                    for k in range(16):
                        MM(ps[:, b, :], mixT[:, k, ti * 128:(ti + 1) * 128], wbuf[sl][:, k, :], k == 0, k == 15,
                           [("wb", sl), "mixT"], [PS(b)], signal=(k == 15))
                    dst = ft[:, ti, nb * 512:(nb + 1) * 512]
                    if (nb + ti) % 2 == 0:
                        AC(dst, ps[:, b, :], AF.Copy, [PS(b)], [("ft", ti)])
                    else:
                        CP("dve", dst, ps[:, b, :], [PS(b)], [("ft", ti)])
                    if nb == 3:
                        s1(ti)
                        if ti in (1, 2, 3):
                            s2a(ti - 1)
            s2b(0)
            s2b(1)
            s2b(2)
            s2a(3)
            s2b(3)
            for q in range(4):
                for jg in range(4):
                    sl = load_w(wup_b, (q * 16 + jg * 4) * 128, 512, ["wup"])
                    if q == 0 and jg == 2 and blk + 1 < NB:
                        load_mixT(blk + 1)
                    for jj in range(4):
                        j = jg * 4 + jj
                        b = bank.next()
                        for k in range(16):
                            MM(ps[:, b, :], wbuf[sl][:, k, jj * 128:(jj + 1) * 128], hfT[:, k, :], k == 0, k == 15,
                               [("wb", sl), ("hfT", k)], [PS(b)], signal=(k == 15))
                        ri = rl_i.next()
                        AC(rl_t[ri], ps[:, b, :], AF.Relu, [PS(b)], [("rl", ri)])
                        TT("pool" if j % 2 == 0 else "dve", aT[:, j, :], rl_t[ri], rl_t[ri], ALU.mult, [("rl", ri)], [("aT", j)])
                for nb in range(4):
                    sl = load_w(wdown_b, nb * 512, 512, ["wdown"], rows0=q * 2048)
                    for ti in range(4):
                        b = bank.next()
                        for j in range(16):
                            MM(ps[:, b, :], aT[:, j, ti * 128:(ti + 1) * 128], wbuf[sl][:, j, :], j == 0, j == 15,
                               [("wb", sl), ("aT", j)], [PS(b)], signal=(j == 15))
                        fsl = ft[:, ti, nb * 512:(nb + 1) * 512]
                        if q == 0:
                            AC(fsl, ps[:, b, :], AF.Copy, [PS(b)], [("ft", ti)])
                        else:
                            TT("dve", fsl, ps[:, b, :], fsl, ALU.add, [PS(b), ("ft", ti)], [("ft", ti)])
            if blk + 1 < NB:
                pre_wout = [load_w(wout_b, nb * 512, 512, ["wout"]) for nb in range(2)]
            for ti in range(4):
                jk, jkk = junk_of(ti)
                c3 = 24 + ti
                rms_stats(ft[:, ti, :], [("ft", ti)], 2048, c3, jk, jkk[0])
                STT(ft[:, ti, :], ft[:, ti, :], sm[:, c3:c3 + 1], wfpost_b[:, :], ALU.mult, ALU.mult,
                    [("ft", ti), ("sm", c3), "wfpost_b"] + jkk[1:], [("ft", ti)])
                TT("dve", ft[:, ti, :], ft[:, ti, :], x1[:, ti, :], ALU.add, [("x1", ti), ("ft", ti)], [("ft", ti)])
                DMA("pool", y[t0 + ti * 128: t0 + (ti + 1) * 128, :], ft[:, ti, :], [("ft", ti)], [("y", t0, ti)], "d_y%d" % ti)
        SC.barrier(keep=WKEYS)

    with nc.Block() as block:
        @block.tensor
        def _(e):
            SC.replay("pe", e)

        @block.scalar
        def _(e):
            SC.replay("act", e)

        @block.vector
        def _(e):
            SC.replay("dve", e)

        @block.gpsimd
        def _(e):
            SC.replay("pool", e)

        @block.sync
        def _(e):
            SC.replay("sp", e)
    return nc


def host_prep(inp):
    f = np.float32

    def colpack(v):
        v = np.asarray(v, f).reshape(-1, 128)
        return np.ascontiguousarray(v.T)
    cols = [colpack(inp["norm_mix_pre"][0]), colpack(inp["norm_ffn_pre"][0])]
    cw = np.asarray(inp["rg_conv_w"][0], f)
    for j in range(4):
        cols.append(colpack(cw[j]))
    cols.append(colpack(inp["rg_conv_b"][0]))
    for nm in ["rg_a_b_fwd", "rg_i_b_fwd", "rg_a_b_bwd", "rg_i_b_bwd", "rg_lambda_fwd", "rg_lambda_bwd"]:
        cols.append(colpack(inp[nm][0]))
    pp = np.ascontiguousarray(np.concatenate(cols, axis=1))
    assert pp.shape == (128, 120)
    bvec = np.concatenate([np.asarray(inp["norm_mix_post"][0], f), np.asarray(inp["norm_ffn_post"][0], f),
                           np.asarray(inp["gla_norm"][0], f)])[None, :]
    gkw = np.zeros((17, 1024), f)
    gkw[:16, :512] = inp["gla_gk_w_fwd"][0]
    gkw[16, :512] = inp["gla_gk_b_fwd"][0]
    gkw[:16, 512:] = inp["gla_gk_w_bwd"][0]
    gkw[16, 512:] = inp["gla_gk_b_bwd"][0]
    rgw = np.concatenate([np.asarray(inp[nm][0], f).reshape(1024, 256)
                          for nm in ["rg_a_w_fwd", "rg_i_w_fwd", "rg_a_w_bwd", "rg_i_w_bwd"]], axis=0)
    i_ = np.arange(128)
    tp, t = i_[:, None], i_[None, :]
    g = -1.0 / 16.0
    cst = np.concatenate([
        np.eye(128, dtype=f),
        (tp <= t).astype(f),
        (tp >= t).astype(f),
        (tp <= t).astype(f) * g,
        (tp >= t).astype(f) * g,
        (tp > t).astype(f) * g,
        (tp < t).astype(f) * g,
    ], axis=1).astype(f)
    return dict(
        w_in=np.ascontiguousarray(inp["w_in"][0], f), w_out=np.ascontiguousarray(inp["w_out"][0], f),
        w_up=np.ascontiguousarray(inp["w_up"][0], f), w_down=np.ascontiguousarray(inp["w_down"][0], f),
        rgw=np.ascontiguousarray(rgw), pp=pp, bvec=np.ascontiguousarray(bvec), gkw=gkw, cst=np.ascontiguousarray(cst))


_NC_CACHE = {}


def kernel(**inputs):
    S = 2048
    NSEQ = 3
    xp = np.asarray(inputs["x_prompt"], np.float32)
    xs = np.asarray(inputs["x_sample"], np.float32)
    allx = np.concatenate([xp, xs], axis=0)
    shared = host_prep(inputs)
    key = (S, NSEQ)
    if key not in _NC_CACHE:
        _NC_CACHE[key] = build(S, NSEQ)
    nc = _NC_CACHE[key]
    in_maps = []
    for c in range(NCORE):
        m = dict(shared)
        m["x"] = np.ascontiguousarray(allx[c * NSEQ:(c + 1) * NSEQ].reshape(NSEQ * S, D))
        in_maps.append(m)
    res = run_bass_kernel_spmd(nc, in_maps, core_ids=list(range(NCORE)))
    ys = np.concatenate([np.asarray(r["y"], np.float32).reshape(NSEQ, S, D) for r in res.results], axis=0)
    return ys[:8].copy(), ys[8:].copy()
```
